# Optimizing a Trainium2 kernel written in Bass

```python
import jax, jax.numpy as jnp
from jax import lax
import numpy as np

D_MODEL = 1024
BATCH = 4
SEQ = 4096
DEPTH = 1

N_Q_HEADS = 16
N_KV_HEADS = 2
HEAD_DIM = 64
Q_PER_KV = N_Q_HEADS // N_KV_HEADS
WINDOW = 128
BLOCK = 128
ROPE_THETA = 500000.0
ROPE_DIM = HEAD_DIM // 4
ATTN_W = N_Q_HEADS * HEAD_DIM
KV_W = N_KV_HEADS * HEAD_DIM
CONV_CH = D_MODEL
CONV_WIDTH = 31
D_FF = 2816
PLE_DIM = 256
EPS = 1e-6
NEG_INF = -1e30
SPLIT_SIZES = (ATTN_W, KV_W, KV_W, 2 * CONV_CH, 2 * D_MODEL)
D_IN = sum(SPLIT_SIZES)

kernel_name = "hybrid_swa_sink_conformer_conv_macaron_ple"


def rmsnorm(x, g):
    xf = x.astype(jnp.float32)
    y = xf * lax.rsqrt(jnp.mean(xf * xf, axis=-1, keepdims=True) + EPS) * g.astype(jnp.float32)
    return y.astype(x.dtype)


def layernorm(x, g, b):
    xf = x.astype(jnp.float32)
    mu = jnp.mean(xf, axis=-1, keepdims=True)
    xc = xf - mu
    var = jnp.mean(xc * xc, axis=-1, keepdims=True)
    y = xc * lax.rsqrt(var + EPS) * g.astype(jnp.float32) + b.astype(jnp.float32)
    return y.astype(x.dtype)


def swiglu(h, w_gu, w_down):
    g, u = jnp.split(h @ w_gu, 2, axis=-1)
    return (jax.nn.silu(g) * u) @ w_down


def partial_rope(t, cos, sin):
    half = ROPE_DIM // 2
    t1 = t[..., :half]
    t2 = t[..., half:ROPE_DIM]
    rot = jnp.concatenate([t1 * cos - t2 * sin, t2 * cos + t1 * sin], axis=-1)
    return jnp.concatenate([rot.astype(t.dtype), t[..., ROPE_DIM:]], axis=-1)


def rope_tables(seq):
    inv_freq = ROPE_THETA ** (-jnp.arange(0, ROPE_DIM, 2, dtype=jnp.float32) / ROPE_DIM)
    ang = jnp.arange(seq, dtype=jnp.float32)[:, None] * inv_freq[None, :]
    return jnp.cos(ang)[None, :, None, :], jnp.sin(ang)[None, :, None, :]


def sliding_window_attention(q, k, v, sinks):
    B, S = q.shape[0], q.shape[1]
    nb = S // BLOCK
    qb = q.reshape(B, nb, BLOCK, N_KV_HEADS, Q_PER_KV, HEAD_DIM)

    def band(t):
        tb = t.reshape(B, nb, BLOCK, N_KV_HEADS, HEAD_DIM)
        prev = jnp.concatenate([jnp.zeros_like(tb[:, :1]), tb[:, :-1]], axis=1)
        return jnp.concatenate([prev, tb], axis=2)

    kb, vb = band(k), band(v)
    scale = HEAD_DIM ** -0.5
    s = jnp.einsum('bnqhgd,bnkhd->bnhgqk', qb, kb,
                   preferred_element_type=jnp.float32) * scale
    qi = jnp.arange(BLOCK)[:, None]
    kj = jnp.arange(2 * BLOCK)[None, :]
    delta = qi + BLOCK - kj
    kpos = jnp.arange(nb)[:, None, None] * BLOCK + kj[None] - BLOCK
    valid = (delta >= 0) & (delta < WINDOW) & (kpos >= 0)
    s = jnp.where(valid[None, :, None, None], s, NEG_INF)
    sink = sinks.astype(jnp.float32).reshape(N_KV_HEADS, Q_PER_KV)[None, None, :, :, None, None]
    m = jnp.maximum(jnp.max(s, axis=-1, keepdims=True), sink)
    pr = jnp.exp(s - m)
    denom = jnp.sum(pr, axis=-1, keepdims=True) + jnp.exp(sink - m)
    o = jnp.einsum('bnhgqk,bnkhd->bnqhgd', pr / denom, vb.astype(jnp.float32))
    return o.reshape(B, S, ATTN_W).astype(q.dtype)


def causal_depthwise_conv(u, w_dw, b_dw):
    up = jnp.pad(u, ((0, 0), (CONV_WIDTH - 1, 0), (0, 0)))
    y = lax.conv_general_dilated(up, w_dw[:, None, :], window_strides=(1,), padding='VALID',
                                 dimension_numbers=('NWC', 'WIO', 'NWC'),
                                 feature_group_count=CONV_CH)
    return y + b_dw


def setup_inputs(seed: int = 0) -> dict:
    key = jax.random.key(seed)
    ks = jax.random.split(key, 32)

    def nrm(k, shape, scale):
        return jax.random.normal(k, shape, jnp.float32) * scale

    def gain(k, shape):
        return 1.0 + 0.05 * jax.random.normal(k, shape, jnp.float32)

    L = DEPTH
    return {
        "x": nrm(ks[0], (BATCH, SEQ, D_MODEL), 1.0),
        "p": nrm(ks[1], (DEPTH, BATCH, SEQ, PLE_DIM), 1.0),
        "ffn1_norm": gain(ks[2], (L, D_MODEL)),
        "ffn1_wgu": nrm(ks[3], (L, D_MODEL, 2 * D_FF), D_MODEL ** -0.5),
        "ffn1_wdown": nrm(ks[4], (L, D_FF, D_MODEL), D_FF ** -0.5),
        "mix_norm": gain(ks[5], (L, D_MODEL)),
        "w_in": nrm(ks[6], (L, D_MODEL, D_IN), D_MODEL ** -0.5),
        "q_norm": gain(ks[7], (L, HEAD_DIM)),
        "k_norm": gain(ks[8], (L, HEAD_DIM)),
        "sinks": nrm(ks[9], (L, N_Q_HEADS), 0.5),
        "attn_proj": nrm(ks[10], (L, ATTN_W, D_MODEL), ATTN_W ** -0.5),
        "dw_w": nrm(ks[11], (L, CONV_WIDTH, CONV_CH), CONV_WIDTH ** -0.5),
        "dw_b": nrm(ks[12], (L, CONV_CH), 0.02),
        "conv_ln_g": gain(ks[13], (L, CONV_CH)),
        "conv_ln_b": nrm(ks[14], (L, CONV_CH), 0.02),
        "conv_proj": nrm(ks[15], (L, CONV_CH, D_MODEL), CONV_CH ** -0.5),
        "w_out": nrm(ks[16], (L, D_MODEL, D_MODEL), D_MODEL ** -0.5),
        "ffn2_norm": gain(ks[17], (L, D_MODEL)),
        "ffn2_wgu": nrm(ks[18], (L, D_MODEL, 2 * D_FF), D_MODEL ** -0.5),
        "ffn2_wdown": nrm(ks[19], (L, D_FF, D_MODEL), D_FF ** -0.5),
        "ple_proj": nrm(ks[20], (L, PLE_DIM, D_MODEL), PLE_DIM ** -0.5),
        "ple_norm": gain(ks[21], (L, D_MODEL)),
        "ple_gate_norm": gain(ks[22], (L, D_MODEL)),
        "ple_gate_w": nrm(ks[23], (L, D_MODEL, D_MODEL), D_MODEL ** -0.5),
    }


def reference(x, p, ffn1_norm, ffn1_wgu, ffn1_wdown, mix_norm, w_in, q_norm, k_norm, sinks,
              attn_proj, dw_w, dw_b, conv_ln_g, conv_ln_b, conv_proj, w_out,
              ffn2_norm, ffn2_wgu, ffn2_wdown, ple_proj, ple_norm, ple_gate_norm, ple_gate_w):
    B, S = x.shape[0], x.shape[1]
    cos, sin = rope_tables(S)
    offs = np.cumsum((0,) + SPLIT_SIZES)
    for i in range(DEPTH):
        x = x + 0.5 * swiglu(rmsnorm(x, ffn1_norm[i]), ffn1_wgu[i], ffn1_wdown[i])

        h = rmsnorm(x, mix_norm[i])
        z = h @ w_in[i]
        zq, zk, zv, zc, zg = [z[..., offs[j]:offs[j + 1]] for j in range(len(SPLIT_SIZES))]

        q = rmsnorm(zq.reshape(B, S, N_Q_HEADS, HEAD_DIM), q_norm[i])
        k = rmsnorm(zk.reshape(B, S, N_KV_HEADS, HEAD_DIM), k_norm[i])
        v = zv.reshape(B, S, N_KV_HEADS, HEAD_DIM)
        q = partial_rope(q, cos, sin)
        k = partial_rope(k, cos, sin)
        a = sliding_window_attention(q, k, v, sinks[i]) @ attn_proj[i]

        ca, cb = jnp.split(zc, 2, axis=-1)
        u = ca * jax.nn.sigmoid(cb)
        u = causal_depthwise_conv(u, dw_w[i], dw_b[i])
        c = jax.nn.silu(layernorm(u, conv_ln_g[i], conv_ln_b[i])) @ conv_proj[i]

        ga, gb = jnp.split(zg, 2, axis=-1)
        merged = jax.nn.sigmoid(ga) * a + jax.nn.sigmoid(gb) * c
        x = x + merged @ w_out[i]

        x = x + 0.5 * swiglu(rmsnorm(x, ffn2_norm[i]), ffn2_wgu[i], ffn2_wdown[i])

        e = rmsnorm(p[i] @ ple_proj[i], ple_norm[i])
        gate = jax.nn.sigmoid(rmsnorm(x, ple_gate_norm[i]) @ ple_gate_w[i])
        x = x + gate * e
    return x
```

```python
import contextlib
import numpy as np
import concourse.bass as bass
import concourse.mybir as mybir
from concourse.bass_utils import run_bass_kernel_spmd

F32 = mybir.dt.float32
BF16 = mybir.dt.bfloat16
AF = mybir.ActivationFunctionType
ALU = mybir.AluOpType

D = 1024
SEQ = 4096
BATCH = 4
NCORES = 8
TOK = 2048
HALO = 128
NT = 512
DFF = 2816
NJ = DFF // 128
EPS = 1e-6
NEG = -30000.0
NSLOT = 4
SLOT_ELEMS = 4096

V_FFN1, V_MIX, V_FFN2, V_PLEN, V_PLEG, V_LNG, V_LNB, V_DWB = [8 * i for i in range(8)]
V_GQ = 64
V_GK = 65
V_MA = 66
V_MB = 67
V_DW = 68
NV = V_DW + 8 * 31

ENGS = ("pe", "act", "dve", "pool", "sp")
CH = 30000


class Op:
    __slots__ = ("eng", "fn", "reads", "writes", "excl", "dma", "deps", "signal", "sig_no",
                 "dma_sem", "dma_use", "tag")

    def __init__(self, eng, fn, reads, writes, excl, dma):
        self.eng, self.fn, self.reads, self.writes, self.excl, self.dma = eng, fn, reads, writes, excl, dma
        self.deps = []
        self.signal = False
        self.sig_no = 0
        self.dma_sem = None
        self.dma_use = 0


class Sched:
    def __init__(self, nc, n_dma_sems=12):
        self.nc = nc
        self.ops = []
        self.last_writer = {}
        self.readers = {}
        self.n_dma_sems = n_dma_sems
        self.tag = ""

    def add(self, eng, fn, reads=(), writes=(), excl=(), dma=False):
        op = Op(eng, fn, tuple(reads), tuple(writes), tuple(excl), dma)
        op.tag = self.tag
        deps = set()
        for r in op.reads:
            w = self.last_writer.get(r)
            if w is not None:
                deps.add(w)
        for r in op.writes + op.excl:
            w = self.last_writer.get(r)
            if w is not None:
                deps.add(w)
            deps.update(self.readers.get(r, ()))
        for r in op.writes + op.excl:
            self.last_writer[r] = op
            self.readers[r] = []
        for r in op.reads:
            self.readers.setdefault(r, []).append(op)
        deps.discard(op)
        touched = set(op.reads) | set(op.writes) | set(op.excl)
        for d in deps:
            if d.eng == op.eng and not d.dma and not op.dma:
                if op.eng == "pe":
                    continue
                if op.eng != "pool" and not (set(d.writes) & touched):
                    continue
            op.deps.append(d)
            d.signal = True
        self.ops.append(op)
        return op

    def emit(self):
        nc = self.nc
        cnt = {e: 0 for e in ENGS}
        dma_cnt = {e: 0 for e in ENGS}
        for op in self.ops:
            if op.dma:
                op.dma_sem = (op.eng, dma_cnt[op.eng] % self.n_dma_sems)
                op.dma_use = dma_cnt[op.eng] // self.n_dma_sems + 1
                dma_cnt[op.eng] += 1
            elif op.signal:
                cnt[op.eng] += 1
                op.sig_no = cnt[op.eng]
        per_eng = {e: [op for op in self.ops if op.eng == e] for e in ENGS}
        with contextlib.ExitStack() as stack:
            sems = {}
            for e in ENGS:
                n = max((cnt[e] + CH - 1) // CH, 1)
                sems[e] = [stack.enter_context(nc.semaphore(f"s_{e}_{i}")) for i in range(n)]
            dsems = {}
            for e in ENGS:
                if dma_cnt[e]:
                    dsems[e] = [stack.enter_context(nc.semaphore(f"d_{e}_{i}"))
                                for i in range(min(self.n_dma_sems, dma_cnt[e]))]

            def run(eng_name, eng):
                known = {}

                def wait(key, sem, val):
                    if known.get(key, 0) >= val:
                        return
                    known[key] = val
                    eng.wait_ge(sem, val)

                for op in per_eng[eng_name]:
                    for d in op.deps:
                        if d.dma:
                            wait(("d",) + d.dma_sem, dsems[d.dma_sem[0]][d.dma_sem[1]], 16 * d.dma_use)
                        else:
                            ci = (d.sig_no - 1) // CH
                            wait((d.eng, ci), sems[d.eng][ci], (d.sig_no - 1) % CH + 1)
                    if op.fn is None:
                        continue
                    if op.dma:
                        s = dsems[op.dma_sem[0]][op.dma_sem[1]]
                        if op.dma_use > 1:
                            wait(("d",) + op.dma_sem, s, 16 * (op.dma_use - 1))
                        op.fn(eng).then_inc(s, 16)
                    else:
                        ins = op.fn(eng)
                        if op.signal:
                            ci = (op.sig_no - 1) // CH
                            ins.then_inc(sems[op.eng][ci], 1)

            with nc.Block() as block:
                @block.tensor
                def _(e):
                    run("pe", e)

                @block.scalar
                def _(e):
                    run("act", e)

                @block.vector
                def _(e):
                    run("dve", e)

                @block.gpsimd
                def _(e):
                    run("pool", e)

                @block.sync
                def _(e):
                    run("sp", e)


def build_program(dbg=False):
    nc = bass.Bass("TRN2", target_bir_lowering=False)

    def din(name, shape):
        return nc.dram_tensor(name, list(shape), F32, kind="ExternalInput").ap()

    xT = din("xT", [D, HALO + TOK])
    pT = din("pT", [256, TOK])
    wgu = [din("wgu1", [D, 11, 512]), din("wgu2", [D, 11, 512])]
    wdn = [din("wdn1", [DFF, D]), din("wdn2", [DFF, D])]
    wconv = din("wconv", [D, 4, 512])
    wq = din("wq", [D, 1024])
    wkv = din("wkv", [D, 384])
    wmerge = din("wmerge", [D, 8, 512])
    wout = din("wout", [D, D])
    wple = din("wple", [256, D])
    wpg = din("wpg", [D, D])
    vecs_d = din("vecs", [128, NV])
    sinks_d = din("sinks_l", [128, 4])
    cs_d = din("cs", [128, 2, HALO + TOK])
    cmat_d = din("cmat", [128, 4, 128])
    cmask_d = din("cmask", [128, 3, 512])
    outT = nc.dram_tensor("outT", [D, TOK], F32, kind="ExternalOutput").ap()
    dbg_out = None
    if dbg:
        dbg_out = [nc.dram_tensor(f"dbg{i}", [D, NT], F32, kind="ExternalOutput").ap() for i in range(4)]

    S = Sched(nc)
    with contextlib.ExitStack() as st:
        def sb(name, shape, dt):
            return st.enter_context(nc.sbuf_tensor(name, list(shape), dt))

        ps = [st.enter_context(nc.psum_tensor(f"ps{i}", [128, 512], F32)) for i in range(8)]
        XB = [sb("X0", [128, 8, NT], F32), sb("X1", [128, 8, NT], F32)]
        curX = [XB[0], 0]
        H = sb("H", [128, 8, NT], BF16)
        R1 = sb("R1", [128, NJ + 2, NT], BF16)
        Kb = [sb(f"Kb{i}", [128, 2, 640], BF16) for i in range(2)]
        Vb = [sb(f"Vb{i}", [128, 4, 128], BF16) for i in range(2)]
        U = sb("U", [128, 8, 30 + NT], BF16)
        DG = [sb(f"DG{i}", [128, 31, 128], BF16) for i in range(2)]
        Y = sb("Y", [128, 8, NT], F32)
        ER = [sb(f"ER{i}", [128, 8, 256], BF16) for i in range(3)]
        CN = sb("CN", [128, 8, NT], BF16)
        CS = sb("CS", [128, 2, NT], F32)
        PT = sb("PT", [128, 2, NT], BF16)
        WS = [sb(f"WS{i}", [128, SLOT_ELEMS], BF16) for i in range(NSLOT)]
        vecs = sb("vecs_sb", [128, NV], F32)
        sinks_t = sb("sinks_t", [128, 4], F32)
        esc = sb("esc", [128, 4], F32)
        cmat = sb("cmat_sb", [128, 4, 128], BF16)
        cmask = sb("cmask_sb", [128, 3, 512], BF16)
        Lm = sb("Lm", [128, 4, 128], BF16)
        EPSC = sb("EPSC", [128, 1], F32)
        ident, ones, blkones, perm = (cmat[:, i, :] for i in range(4))

        ACTB = lambda j: R1[:, j, :]
        QA = lambda c: R1[:, c, :]
        QB = lambda c: R1[:, 8 + c, :]

        def ATv(p0, p1, j0, j1, q0, q1):
            return R1[p0:p1, 16 + j0:16 + j1, q0:q1]

        tmps = {}
        SHARED = {"SG": "F", "QR": "F", "TB": "F", "RD": "F", "T1": "F", "TGA": "F", "TGB": "F", "M1": "F", "M2": "F",
                  "VE": "F", "MSQ": "F", "SQ": "B", "A": "B", "AC": "B", "AS": "B", "YB": "B", "YS": "B"}
        RING_N = {"F": 8, "B": 6, "RS": 2, "RSQ": 2, "NMU": 1}

        def tmp(name, dt, n=2):
            rn = SHARED.get(name, name)
            if rn not in tmps:
                rdt = F32 if rn != "B" else BF16
                assert rdt == dt, (name, dt)
                tmps[rn] = [[sb(f"t_{rn}{i}", [128, NT], rdt) for i in range(RING_N[rn])], 0]
            lst = tmps[rn]
            i = lst[1] % len(lst[0])
            lst[1] += 1
            return lst[0][i], ("T", rn, i)

        bank_ctr = [0]

        def nb(excluded=()):
            while True:
                b = bank_ctr[0] % 8
                bank_ctr[0] += 1
                if b not in excluded:
                    return b

        P = lambda b: ("ps", b)

        wblocks = []
        wstate = {"emitted": 0, "next": 0}

        def plan_blocks():
            def tile_blocks(halo):
                bl = []

                def ffn(i):
                    for jb in range(11):
                        if i == 1 and jb == 4:
                            bl.append(("ple", wple.rearrange("(kc p) c -> p kc c", p=128), [2, 1024]))
                        bl.append(("gu", wgu[i].rearrange("(kc p) j c -> p kc j c", p=128)[:, :, jb, :], [8, 512]))
                    for jb in range(11):
                        bl.append(("dn", wdn[i].rearrange("(j p) m -> p j m", p=128)[:, 2 * jb:2 * jb + 2, :], [2, 1024]))
                ffn(0)
                for cb in range(4):
                    bl.append(("conv", wconv.rearrange("(kc p) j c -> p kc j c", p=128)[:, :, cb, :], [8, 512]))
                if not halo:
                    for qb in range(2):
                        bl.append(("q", wq.rearrange("(kc p) c -> p kc c", p=128)[:, :, qb * 512:(qb + 1) * 512], [8, 512]))
                bl.append(("kv", wkv.rearrange("(kc p) c -> p kc c", p=128), [8, 384]))
                if not halo:
                    for c in range(8):
                        bl.append(("mg", wmerge.rearrange("(kc p) j c -> p kc j c", p=128)[:, :, c, :], [8, 512]))
                    for ob in range(2):
                        bl.append(("wo", wout.rearrange("(kc p) c -> p kc c", p=128)[:, :, ob * 512:(ob + 1) * 512], [8, 512]))
                    ffn(1)
                    for ob in range(2):
                        bl.append(("pg", wpg.rearrange("(kc p) c -> p kc c", p=128)[:, :, ob * 512:(ob + 1) * 512], [8, 512]))
                return bl
            wblocks.extend(tile_blocks(True))
            for _ in range(TOK // NT):
                wblocks.extend(tile_blocks(False))

        plan_blocks()

        def wview(i):
            kind, src, (a, b) = wblocks[i]
            return WS[i % NSLOT][:, 0:a * b].rearrange("p (a b) -> p a b", a=a)

        def wnext(kind):
            i = wstate["next"]
            assert wblocks[i][0] == kind, (wblocks[i][0], kind, i)
            wstate["next"] += 1
            while wstate["emitted"] < min(len(wblocks), i + NSLOT):
                j = wstate["emitted"]
                dst = wview(j)
                src = wblocks[j][1]
                S.add("pool", (lambda d, s_: lambda e: e.dma_start(out=d, in_=s_))(dst, src),
                      writes=[("W", j % NSLOT)], dma=True)
                wstate["emitted"] += 1
            return wview(i), ("W", i % NSLOT)

        S.add("sp", lambda e: e.dma_start(out=vecs[:], in_=vecs_d), writes=["vecs"], dma=True)
        S.add("sp", lambda e: e.dma_start(out=sinks_t[:], in_=sinks_d), writes=["sinks_t"], dma=True)
        S.add("pool", lambda e: e.dma_start(out=cmat[:], in_=cmat_d), writes=["cmat"], dma=True)
        S.add("pool", lambda e: e.dma_start(out=cmask[:], in_=cmask_d), writes=["cmask"], dma=True)
        S.add("dve", lambda e: e.memset(EPSC[:], EPS), writes=["EPSC"])
        S.add("act", lambda e: e.activation(out=esc[:], in_=sinks_t[:], func=AF.Exp), reads=["sinks_t"], writes=["esc"])
        for m in range(4):
            S.add("dve", (lambda m: lambda e: e.tensor_scalar(out=Lm[:, m, :], in0=ones, scalar1=esc[:, m:m + 1], scalar2=None, op0=ALU.mult))(m),
                  reads=["esc", "cmat"], writes=[("Lm", m)])
        S.add("dve", lambda e: e.tensor_scalar(out=vecs[:, V_DW:NV], in0=vecs[:, V_DW:NV], scalar1=0.5, scalar2=None, op0=ALU.mult),
              reads=["vecs"], writes=["vecs"])


        def MM(out, lhsT, rhs, start, stop, skip=False):
            return lambda e: e.matmul(out, lhsT=lhsT, rhs=rhs, start=start, stop=stop, skip_group_check=skip)

        def ACTF(out, in_, func, scale=None, bias=None):
            kw = {}
            if scale is not None:
                kw["scale"] = scale
            if bias is not None:
                kw["bias"] = bias
            return lambda e: e.activation(out=out, in_=in_, func=func, **kw)

        def TT(out, in0, in1, op):
            return lambda e: e.tensor_tensor(out=out, in0=in0, in1=in1, op=op)

        def STT(out, in0, scalar, in1, op0, op1):
            return lambda e: e.scalar_tensor_tensor(out=out, in0=in0, scalar=scalar, in1=in1, op0=op0, op1=op1)

        def TS(out, in0, s1, s2, op0, op1=None):
            if op1 is None:
                return lambda e: e.tensor_scalar(out=out, in0=in0, scalar1=s1, scalar2=None, op0=op0)
            return lambda e: e.tensor_scalar(out=out, in0=in0, scalar1=s1, scalar2=s2, op0=op0, op1=op1)

        def DMA(out, in_):
            return lambda e: e.dma_start(out=out, in_=in_)

        R1k = lambda i: ("R1", i)
        Xk = lambda kc: ("X", curX[1], kc)
        Hk = lambda kc: ("H", kc)

        def rstd_from_psum(bank, n, inv_d, ring="RS"):
            LN_, kLN = tmp("VE", F32)
            RS, kRS = tmp(ring, F32)
            S.add("act", ACTF(LN_[:, 0:n], ps[bank][:, 0:n], AF.Ln, scale=inv_d, bias=EPSC[:, 0:1]), excl=[P(bank)], reads=["EPSC"], writes=[kLN])
            S.add("act", ACTF(RS[:, 0:n], LN_[:, 0:n], AF.Exp, scale=-0.5), reads=[kLN], writes=[kRS])
            return RS, kRS

        def rmsnorm_to_H(n, vcol):
            b = nb()
            for kc in range(8):
                SQ, kSQ = tmp("SQ", BF16, 3)
                S.add("act", ACTF(SQ[:, 0:n], curX[0][:, kc, 0:n], AF.Square), reads=[Xk(kc)], writes=[kSQ])
                S.add("pe", MM(ps[b][:, 0:n], ones, SQ[:, 0:n], kc == 0, kc == 7), reads=[kSQ, "cmat"], excl=[P(b)])
            RS, kRS = rstd_from_psum(b, n, 1.0 / D)
            for kc in range(8):
                S.add("dve", STT(H[:, kc, 0:n], curX[0][:, kc, 0:n], vecs[:, vcol + kc:vcol + kc + 1], RS[:, 0:n], ALU.mult, ALU.mult),
                      reads=[Xk(kc), kRS, "vecs"], writes=[Hk(kc)])

        def proj8(W, kW, col0, rhs_fn, rhs_key_fn, n, bank=None, extra_reads=()):
            b = nb() if bank is None else bank
            for kc in range(8):
                S.add("pe", MM(ps[b][:, 0:n], W[:, kc, col0:col0 + 128], rhs_fn(kc), kc == 0, kc == 7),
                      reads=[kW, rhs_key_fn(kc)], excl=[P(b)])
            return b

        def ffn(n, vcol, tg='', mid_hook=None):
            S.tag = tg + '.norm'
            X_ = curX[0]
            for kc in range(8):
                S.add("dve", TS(H[:, kc, 0:n], X_[:, kc, 0:n], vecs[:, vcol + kc:vcol + kc + 1], None, ALU.mult),
                      reads=[Xk(kc), "vecs"], writes=[Hk(kc)])
            bst = nb()
            for kc in range(8):
                SQ, kSQ = tmp("SQ", BF16, 3)
                S.add("act", ACTF(SQ[:, 0:n], X_[:, kc, 0:n], AF.Square), reads=[Xk(kc)], writes=[kSQ])
                S.add("pe", MM(ps[bst][:, 0:n], ones, SQ[:, 0:n], kc == 0, kc == 7), reads=[kSQ, "cmat"], excl=[P(bst)])
            RS, kRS = rstd_from_psum(bst, n, 1.0 / D)
            S.tag = tg + '.gu'
            Hr = lambda kc: H[:, kc, 0:n]
            for jb in range(11):
                if mid_hook is not None and jb == 4:
                    mid_hook()
                    S.tag = tg + '.gu'
                W, kW = wnext("gu")
                for jj in range(2):
                    j = 2 * jb + jj
                    bg = proj8(W, kW, jj * 128, Hr, Hk, n)
                    bu = proj8(W, kW, 256 + jj * 128, Hr, Hk, n)
                    G1, kG1 = tmp("T1", F32)
                    SG, kSG = tmp("SG", F32)
                    S.add("dve", TT(G1[:, 0:n], ps[bg][:, 0:n], RS[:, 0:n], ALU.mult), reads=[kRS], excl=[P(bg)], writes=[kG1])
                    S.add("act", ACTF(SG[:, 0:n], G1[:, 0:n], AF.Silu), reads=[kG1], writes=[kSG])
                    S.add("dve", TT(R1[:, j, 0:n], SG[:, 0:n], ps[bu][:, 0:n], ALU.mult), reads=[kSG], excl=[P(bu)], writes=[R1k(j)])
            S.tag = tg + '.down'
            for jb in range(11):
                W, kW = wnext("dn")
                for jj in range(2):
                    j = 2 * jb + jj
                    for m in range(8):
                        S.add("pe", MM(ps[m][:, 0:n], W[:, jj, m * 128:(m + 1) * 128], R1[:, j, 0:n], j == 0, j == NJ - 1),
                              reads=[kW, R1k(j)], excl=[P(m)])
            for m in range(8):
                DT, kDT = tmp("M1", F32)
                S.add("dve", STT(DT[:, 0:n], ps[m][:, 0:n], 0.5, RS[:, 0:n], ALU.mult, ALU.mult), reads=[kRS], excl=[P(m)], writes=[kDT])
                S.add("pool", TT(X_[:, m, 0:n], DT[:, 0:n], X_[:, m, 0:n], ALU.add), reads=[kDT], writes=[Xk(m)])

        def qk_pipeline(items, n):
            st_ = {}

            def s1(i):
                proj_fn, vg, dst_fn = items[i]
                bank = proj_fn()
                SQ, kSQ = tmp("SQ", BF16)
                S.add("act", ACTF(SQ[:, 0:n], ps[bank][:, 0:n], AF.Square), excl=[P(bank)], writes=[kSQ])
                st_[i] = (bank, SQ, kSQ)

            def s2(i):
                proj_fn, vg, dst_fn = items[i]
                bank, SQ, kSQ = st_[i]
                b2 = nb()
                S.add("pe", MM(ps[b2][:, 0:n], blkones, SQ[:, 0:n], True, True), reads=[kSQ, "cmat"], excl=[P(b2)])
                RS, kRS = rstd_from_psum(b2, n, 1.0 / 64, ring="RSQ")
                A, kA = tmp("A", BF16)
                S.add("dve", STT(A[:, 0:n], ps[bank][:, 0:n], vecs[:, vg:vg + 1], RS[:, 0:n], ALU.mult, ALU.mult), reads=[kRS, "vecs"], excl=[P(bank)], writes=[kA])
                AC, kAC = tmp("AC", BF16)
                AS, kAS = tmp("AS", BF16)
                S.add("dve", TT(AC[:, 0:n], A[:, 0:n], CS[:, 0, 0:n], ALU.mult), reads=[kA, "CS"], writes=[kAC])
                S.add("pool", TT(AS[:, 0:n], A[:, 0:n], CS[:, 1, 0:n], ALU.mult), reads=[kA, "CS"], writes=[kAS])
                st_[i] = (AC, kAC, AS, kAS)

            def s3(i):
                proj_fn, vg, dst_fn = items[i]
                AC, kAC, AS, kAS = st_[i]
                b3 = nb()
                S.add("pe", MM(ps[b3][:, 0:n], ident, AC[:, 0:n], True, False), reads=[kAC, "cmat"], excl=[P(b3)])
                S.add("pe", MM(ps[b3][:, 0:n], perm, AS[:, 0:n], False, True), reads=[kAS, "cmat"], excl=[P(b3)])
                dst_fn(b3)

            ni = len(items)
            for step in range(ni + 2):
                if step < ni:
                    s1(step)
                if 0 <= step - 1 < ni:
                    s2(step - 1)
                if 0 <= step - 2 < ni:
                    s3(step - 2)

        def tile(ti):
            halo = ti == 0
            n = HALO if halo else NT
            c0 = 0 if halo else HALO + (ti - 1) * NT
            par = ti % 2
            Hr = lambda kc: H[:, kc, 0:n]
            curX[0], curX[1] = XB[par], par
            allX = [Xk(kc) for kc in range(8)]
            S.tag = f'{ti}.load'

            def emit_xload(tj):
                nj = HALO if tj == 0 else NT
                cj = 0 if tj == 0 else HALO + (tj - 1) * NT
                pj = tj % 2
                S.add("sp", DMA(XB[pj][:, :, 0:nj], xT.rearrange("(kc p) n -> p kc n", p=128)[:, :, cj:cj + nj]),
                      writes=[("X", pj, kc) for kc in range(8)], dma=True)
            if ti == 0:
                emit_xload(0)
            S.add("sp", DMA(CS[:, :, 0:n], cs_d[:, :, c0:c0 + n]), writes=["CS"], dma=True)
            if not halo:
                S.add("pool", DMA(PT[:], pT.rearrange("(kc p) n -> p kc n", p=128)[:, :, c0 - HALO:c0 - HALO + NT]), writes=["PT"], dma=True)
            ffn(n, V_FFN1, f'{ti}.ffn1')
            if ti + 1 <= TOK // NT:
                S.tag = f'{ti}.load'
                emit_xload(ti + 1)
            if dbg and ti == 1:
                S.add("sp", DMA(dbg_out[0].rearrange("(kc p) n -> p kc n", p=128), curX[0][:]), reads=allX, writes=["dbg0"], dma=True)
            S.tag = f'{ti}.mixnorm'
            rmsnorm_to_H(n, V_MIX)
            S.tag = f'{ti}.convproj'
            for cb in range(4):
                W, kW = wnext("conv")
                for cc in range(2):
                    c = 2 * cb + cc
                    ba = proj8(W, kW, cc * 128, Hr, Hk, n)
                    bb = proj8(W, kW, 256 + cc * 128, Hr, Hk, n)
                    TB, kTB = tmp("TB", F32)
                    if halo:
                        S.add("act", ACTF(TB[:, 0:30], ps[bb][:, n - 30:n], AF.Tanh, scale=0.5), excl=[P(bb)], writes=[kTB])
                        S.add("dve", STT(U[:, c, 0:30], TB[:, 0:30], 1.0, ps[ba][:, n - 30:n], ALU.add, ALU.mult), reads=[kTB], excl=[P(ba)], writes=[("U", c)])
                    else:
                        S.add("act", ACTF(TB[:, 0:n], ps[bb][:, 0:n], AF.Tanh, scale=0.5), excl=[P(bb)], writes=[kTB])
                        S.add("dve", STT(U[:, c, 30:30 + n], TB[:, 0:n], 1.0, ps[ba][:, 0:n], ALU.add, ALU.mult), reads=[kTB], excl=[P(ba)], writes=[("U", c)])
            S.tag = f'{ti}.q'
            if not halo:
                pass
            items = []
            wq_state = {}

            def mk_q(c):
                def proj():
                    qb_, cc = divmod(c, 4)
                    if cc == 0:
                        wq_state["W"] = wnext("q")
                    W, kW = wq_state["W"]
                    return proj8(W, kW, cc * 128, Hr, Hk, n)

                def qdst(b3):
                    S.add("act", ACTF(R1[:, c, 0:n], ps[b3][:, 0:n], AF.Copy, scale=vecs[:, V_MA:V_MA + 1]), reads=["vecs"], excl=[P(b3)], writes=[R1k(c)])
                    S.add("dve", TS(R1[:, 8 + c, 0:n], ps[b3][:, 0:n], vecs[:, V_MB:V_MB + 1], None, ALU.mult), reads=["vecs"], excl=[P(b3)], writes=[R1k(8 + c)])
                return (proj, V_GQ, qdst)

            kcol0 = 512 if halo else 128

            def mk_k(g):
                def proj():
                    if g == 0:
                        wq_state["KV"] = wnext("kv")
                    W, kW = wq_state["KV"]
                    return proj8(W, kW, g * 128, Hr, Hk, n)

                def kdst(b3):
                    S.add("act", ACTF(Kb[par][:, g, kcol0:kcol0 + n], ps[b3][:, 0:n], AF.Copy), excl=[P(b3)], writes=[("K", par, g)])
                return (proj, V_GK, kdst)

            if not halo:
                items += [mk_q(c) for c in range(8)]
            items += [mk_k(g) for g in range(2)]
            qk_pipeline(items, n)
            S.tag = f'{ti}.kv'
            W, kW = wq_state["KV"]
            for b in range(n // 128):
                bv = nb()
                vslot = 3 if halo else b
                for kc in range(8):
                    S.add("pe", MM(ps[bv][:, 0:128], H[:, kc, b * 128:(b + 1) * 128], W[:, kc, 256:384], kc == 0, kc == 7),
                          reads=[kW, Hk(kc)], excl=[P(bv)])
                S.add("act", ACTF(Vb[par][:, vslot, :], ps[bv][:, 0:128], AF.Copy), excl=[P(bv)], writes=[("V", par, vslot)])
            if halo:
                return
            S.tag = f'{ti}.conv'
            for c in range(8):
                wc = V_DW + c * 31
                dg = DG[c % 2]
                S.add("dve", TT(dg[:], ident.unsqueeze(1).broadcast_to([128, 31, 128]),
                                 vecs[:, wc:wc + 31].unsqueeze(2).broadcast_to([128, 31, 128]), ALU.mult),
                      reads=["cmat", "vecs"], writes=[("DG", c % 2)])
                bcv = nb()
                for j in range(31):
                    S.add("pe", MM(ps[bcv][:, :], dg[:, j, :], U[:, c, j:j + NT], j == 0, j == 30), reads=[("DG", c % 2), ("U", c)], excl=[P(bcv)])
                S.add("act", ACTF(Y[:, c, :], ps[bcv][:, :], AF.Identity, bias=vecs[:, V_DWB + c:V_DWB + c + 1]), reads=["vecs"], excl=[P(bcv)], writes=[("Y", c)])
                S.add("pool", (lambda c: lambda e: e.tensor_copy(out=U[:, c, 0:30], in_=U[:, c, NT:NT + 30]))(c), writes=[("U", c)])
            S.tag = f'{ti}.ln'
            b1 = nb()
            b2 = nb()
            for c in range(8):
                YB, kYB = tmp("YB", BF16)
                YS, kYS = tmp("YS", BF16)
                S.add("act", ACTF(YB[:], Y[:, c, :], AF.Copy), reads=[("Y", c)], writes=[kYB])
                S.add("act", ACTF(YS[:], Y[:, c, :], AF.Square), reads=[("Y", c)], writes=[kYS])
                S.add("pe", MM(ps[b1][:, :], ones, YB[:], c == 0, c == 7), reads=[kYB, "cmat"], excl=[P(b1)])
                S.add("pe", MM(ps[b2][:, :], ones, YS[:], c == 0, c == 7), reads=[kYS, "cmat"], excl=[P(b2)])
            NMU, kNMU = tmp("NMU", F32, 1)
            MSQ, kMSQ = tmp("MSQ", F32, 1)
            VE, kVE = tmp("VE", F32)
            RS, kRS = tmp("RS", F32)
            S.add("dve", TS(NMU[:], ps[b1][:, :], -1.0 / D, None, ALU.mult), excl=[P(b1)], writes=[kNMU])
            S.add("dve", TT(MSQ[:], NMU[:], NMU[:], ALU.mult), reads=[kNMU], writes=[kMSQ])
            S.add("dve", STT(VE[:], ps[b2][:, :], 1.0 / D, MSQ[:], ALU.mult, ALU.subtract), reads=[kMSQ], excl=[P(b2)], writes=[kVE])
            S.add("act", ACTF(VE[:], VE[:], AF.Ln, scale=1.0, bias=EPSC[:, 0:1]), reads=[kVE, "EPSC"], writes=[kVE])
            S.add("act", ACTF(RS[:], VE[:], AF.Exp, scale=-0.5), reads=[kVE], writes=[kRS])
            def ln_normalize():
                S.tag = f'{ti}.ln2'
                for c in range(8):
                    T1, kT1 = tmp("T1", F32)
                    S.add("dve", TT(T1[:], Y[:, c, :], NMU[:], ALU.add), reads=[("Y", c), kNMU], writes=[kT1])
                    S.add("dve", TT(T1[:], T1[:], RS[:], ALU.mult), reads=[kT1, kRS], writes=[kT1])
                    S.add("act", ACTF(CN[:, c, :], T1[:], AF.Silu, scale=vecs[:, V_LNG + c:V_LNG + c + 1], bias=vecs[:, V_LNB + c:V_LNB + c + 1]),
                          reads=[kT1, "vecs"], writes=[("CN", c)])
                S.tag = f'{ti}.attn'
            S.tag = f'{ti}.attn'
            ecnt = [0]
            for g in range(2):
                Eprev = None
                if g == 1:
                    ln_normalize()
                for kbl in range(-1, 4):
                    q_lo = max(kbl, 0) * 128
                    q_hi = min(kbl + 2, 4) * 128
                    ncols = q_hi - q_lo
                    rel0 = q_lo - kbl * 128
                    if kbl == -1:
                        kT = Kb[1 - par][:, g, 512:640]
                        kkey = ("K", 1 - par, g)
                    else:
                        kT = Kb[par][:, g, 128 + kbl * 128:256 + kbl * 128]
                        kkey = ("K", par, g)
                    mi = 1 if (kbl == -1 and ti == 1) else 0
                    er = ecnt[0] % 3
                    ecnt[0] += 1
                    Et = ER[er]
                    for pp in range(4):
                        c = 4 * g + pp
                        b = nb()
                        pv = ps[b][:, :].rearrange("p (h q) -> p h q", h=2)[:, :, rel0:rel0 + ncols]
                        mv = cmask[:, mi, :].rearrange("p (h q) -> p h q", h=2)[:, :, rel0:rel0 + ncols]
                        if ncols == 256:
                            S.add("pe", MM(ps[b][:, :], ident, cmask[:, mi, :], True, False, skip=True), reads=["cmat", "cmask"], excl=[P(b)])
                        else:
                            S.add("pe", MM(ps[b][:, rel0:rel0 + ncols], ident, cmask[:, mi, rel0:rel0 + ncols], True, False, skip=True),
                                  reads=["cmat", "cmask"], excl=[P(b)])
                            S.add("pe", MM(ps[b][:, 256 + rel0:256 + rel0 + ncols], ident, cmask[:, mi, 256 + rel0:256 + rel0 + ncols], False, False, skip=True),
                                  reads=["cmat", "cmask"], excl=[P(b)])
                        S.add("pe", MM(ps[b][:, rel0:rel0 + ncols], kT, R1[:, c, q_lo:q_hi], False, False, skip=True),
                              reads=[kkey, R1k(c)], excl=[P(b)])
                        S.add("pe", MM(ps[b][:, 256 + rel0:256 + rel0 + ncols], kT, R1[:, 8 + c, q_lo:q_hi], False, True, skip=True),
                              reads=[kkey, R1k(8 + c)], excl=[P(b)])
                        S.add("act", ACTF(Et[:, 2 * pp:2 * pp + 2, rel0:rel0 + ncols], pv, AF.Exp, scale=0.125), excl=[P(b)], writes=[("E", er, pp)])
                    if kbl >= 0:
                        qb = kbl
                        if kbl == 0:
                            Vprev, kVp = Vb[1 - par][:, 3, :], ("V", 1 - par, 3)
                        else:
                            Vprev, kVp = Vb[par][:, kbl - 1, :], ("V", par, kbl - 1)
                        Vcur, kVc = Vb[par][:, kbl, :], ("V", par, kbl)
                        erp = Eprev
                        for half in range(2):
                            bn_ = nb()
                            bd_ = nb()
                            ep = ER[erp][:, 4 * half:4 * half + 4, 128:256]
                            ec = Et[:, 4 * half:4 * half + 4, 0:128]
                            ekeys = [("E", erp, 2 * half), ("E", erp, 2 * half + 1), ("E", er, 2 * half), ("E", er, 2 * half + 1)]
                            pn = ps[bn_][:, :].rearrange("p (h q) -> p h q", h=4)
                            pd = ps[bd_][:, :].rearrange("p (h q) -> p h q", h=4)
                            S.add("pe", MM(ps[bn_][:, :], Vprev, ep, True, False), reads=[kVp] + ekeys, excl=[P(bn_)])
                            S.add("pe", MM(ps[bn_][:, :], Vcur, ec, False, True), reads=[kVc] + ekeys, excl=[P(bn_)])
                            S.add("pe", MM(ps[bd_][:, :], ones, ep, True, False), reads=["cmat"] + ekeys, excl=[P(bd_)])
                            S.add("pe", MM(ps[bd_][:, :], ones, ec, False, False), reads=["cmat"] + ekeys, excl=[P(bd_)])
                            mm = 2 * g + half
                            S.add("pe", MM(ps[bd_][:, :], Lm[:, mm, :], cmask[:, 2, :], False, True), reads=[("Lm", mm), "cmask"], excl=[P(bd_)])
                            RD, kRD = tmp("RD", F32)
                            r0, r1 = g * 64, (g + 1) * 64
                            S.add("act", ACTF(RD[r0:r1, :], ps[bd_][r0:r1, :], AF.Ln), excl=[P(bd_)], writes=[kRD])
                            S.add("act", ACTF(RD[r0:r1, :], RD[r0:r1, :], AF.Exp, scale=-1.0), reads=[kRD], writes=[kRD])
                            S.add("dve", TT(R1[r0:r1, 16 + 4 * half:16 + 4 * half + 4, qb * 128:(qb + 1) * 128], pn[r0:r1, :, :],
                                            RD[r0:r1, :].rearrange("p (h q) -> p h q", h=4), ALU.mult),
                                  reads=[kRD], excl=[P(bn_)], writes=[R1k(16 + 4 * half + jj) for jj in range(4)])
                    Eprev = er
            S.tag = f'{ti}.merge'
            for c in range(8):
                W, kW = wnext("mg")
                b_ga = proj8(W, kW, 256, Hr, Hk, NT)
                b_gb = proj8(W, kW, 384, Hr, Hk, NT)
                b_c = proj8(W, kW, 128, lambda kc: CN[:, kc, :], lambda kc: ("CN", kc), NT)
                b_a = proj8(W, kW, 0, lambda kc: R1[:, 16 + kc, :], lambda kc: R1k(16 + kc), NT)
                TGA, kTGA = tmp("TGA", F32)
                TGB, kTGB = tmp("TGB", F32)
                M1, kM1 = tmp("M1", F32)
                M2, kM2 = tmp("M2", F32)
                S.add("act", ACTF(TGA[:], ps[b_ga][:, :], AF.Tanh, scale=0.5), excl=[P(b_ga)], writes=[kTGA])
                S.add("act", ACTF(TGB[:], ps[b_gb][:, :], AF.Tanh, scale=0.5), excl=[P(b_gb)], writes=[kTGB])
                S.add("dve", STT(M1[:], TGA[:], 1.0, ps[b_a][:, :], ALU.add, ALU.mult), reads=[kTGA], excl=[P(b_a)], writes=[kM1])
                S.add("dve", STT(M2[:], TGB[:], 1.0, ps[b_c][:, :], ALU.add, ALU.mult), reads=[kTGB], excl=[P(b_c)], writes=[kM2])
                S.add("pool", TT(R1[:, c, :], M1[:], M2[:], ALU.add), reads=[kM1, kM2], writes=[R1k(c)])
            S.tag = f'{ti}.wout'
            for ob in range(2):
                W, kW = wnext("wo")
                for mm in range(4):
                    m = 4 * ob + mm
                    b = proj8(W, kW, mm * 128, lambda kc: R1[:, kc, :], lambda kc: R1k(kc), NT)
                    S.add("dve", STT(curX[0][:, m, :], ps[b][:, :], 0.5, curX[0][:, m, :], ALU.mult, ALU.add), excl=[P(b)], writes=[Xk(m)])
            if dbg and ti == 1:
                S.add("sp", DMA(dbg_out[1].rearrange("(kc p) n -> p kc n", p=128), curX[0][:]), reads=allX, writes=["dbg1"], dma=True)
            rse_box = {}

            def ple_e_part():
                S.tag = f'{ti}.ple_e'
                W, kW = wnext("ple")
                bss = nb()
                pend = []
                for m in range(8):
                    b = nb(excluded=(bss,))
                    for kc in range(2):
                        S.add("pe", MM(ps[b][:, :], W[:, kc, m * 128:(m + 1) * 128], PT[:, kc, :], kc == 0, kc == 1), reads=[kW, "PT"], excl=[P(b)])
                    SQ, kSQ = tmp("SQ", BF16, 3)
                    S.add("act", ACTF(Y[:, m, :], ps[b][:, :], AF.Copy), excl=[P(b)], writes=[("Y", m)])
                    S.add("act", ACTF(SQ[:], ps[b][:, :], AF.Square), excl=[P(b)], writes=[kSQ])
                    pend.append((SQ, kSQ, m))
                    if len(pend) > 2:
                        SQ_, kSQ_, m_ = pend.pop(0)
                        S.add("pe", MM(ps[bss][:, :], ones, SQ_[:], m_ == 0, m_ == 7), reads=[kSQ_, "cmat"], excl=[P(bss)])
                for SQ_, kSQ_, m_ in pend:
                    S.add("pe", MM(ps[bss][:, :], ones, SQ_[:], m_ == 0, m_ == 7), reads=[kSQ_, "cmat"], excl=[P(bss)])
                RSe, kRSe = rstd_from_psum(bss, NT, 1.0 / D)
                rse_box['v'] = (RSe, kRSe)
                for m in range(8):
                    S.add("dve", STT(Y[:, m, :], Y[:, m, :], vecs[:, V_PLEN + m:V_PLEN + m + 1], RSe[:], ALU.mult, ALU.mult),
                          reads=[kRSe, "vecs"], writes=[("Y", m)])
            ffn(n, V_FFN2, f'{ti}.ffn2', mid_hook=ple_e_part)
            if dbg and ti == 1:
                S.add("sp", DMA(dbg_out[2].rearrange("(kc p) n -> p kc n", p=128), curX[0][:]), reads=allX, writes=["dbg2"], dma=True)
            S.tag = f'{ti}.ple'
            rmsnorm_to_H(NT, V_PLEG)
            for ob in range(2):
                W, kW = wnext("pg")
                for mm in range(4):
                    m = 4 * ob + mm
                    b = proj8(W, kW, mm * 128, Hr, Hk, NT)
                    TG, kTG = tmp("TGA", F32)
                    E1, kE1 = tmp("M1", F32)
                    S.add("act", ACTF(TG[:], ps[b][:, :], AF.Tanh, scale=0.5), excl=[P(b)], writes=[kTG])
                    S.add("dve", STT(E1[:], TG[:], 1.0, Y[:, m, :], ALU.add, ALU.mult), reads=[kTG, ("Y", m)], writes=[kE1])
                    S.add("dve", STT(curX[0][:, m, :], E1[:], 0.5, curX[0][:, m, :], ALU.mult, ALU.add), reads=[kE1], writes=[Xk(m)])
            S.tag = f'{ti}.store'
            o0 = c0 - HALO
            S.add("sp", DMA(outT.rearrange("(kc p) n -> p kc n", p=128)[:, :, o0:o0 + NT], curX[0][:]), reads=allX, writes=[("out", ti)], dma=True)

        for ti in range(1 + TOK // NT):
            tile(ti)
        S.add("sp", None, reads=[("out", ti) for ti in range(1, 1 + TOK // NT)] + (["dbg0", "dbg1", "dbg2"] if dbg else []))
        assert wstate["next"] == len(wblocks), (wstate, len(wblocks))
        print('sbuf bytes remaining', nc.sbuf_bytes_remaining)
        S.emit()
    global _LAST_S
    _LAST_S = S
    return nc


def _col(v):
    return np.ascontiguousarray(np.asarray(v, np.float32).reshape(8, 128).T)


def _host_consts():
    ident = np.eye(128, dtype=np.float32)
    ones = np.ones((128, 128), np.float32)
    blk = np.zeros((128, 128), np.float32)
    blk[0:64, 0:64] = 1
    blk[64:128, 64:128] = 1
    perm = np.zeros((128, 128), np.float32)
    for h in range(2):
        for d in range(8):
            perm[h * 64 + d + 8, h * 64 + d] = -1.0
            perm[h * 64 + d, h * 64 + d + 8] = 1.0
    cmat = np.stack([ident, ones, blk, perm], axis=1)
    k = np.arange(128)[:, None]
    q = np.arange(256)[None, :]
    delta = q - k
    m = np.where((delta >= 0) & (delta < 128), 0.0, NEG).astype(np.float32)
    maskb = np.concatenate([m, m], axis=1)
    rind = np.zeros((128, 512), np.float32)
    for r in range(4):
        rind[r, r * 128:(r + 1) * 128] = 1.0
    return cmat, maskb, rind


def _rope_tables(pos):
    inv_freq = (500000.0 ** (-(np.arange(0, 16, 2, dtype=np.float32)) / np.float32(16))).astype(np.float32)
    ang = pos.astype(np.float32)[None, :] * inv_freq[:, None]
    cos = np.cos(ang).astype(np.float32)
    sin = np.sin(ang).astype(np.float32)
    C = np.ones((128, pos.shape[0]), np.float32)
    Sn = np.zeros((128, pos.shape[0]), np.float32)
    for h in range(2):
        for d in range(16):
            C[h * 64 + d] = cos[d % 8]
            Sn[h * 64 + d] = sin[d % 8]
    return C, Sn


_NC_CACHE = {}
_LAST_S = None


def kernel(x, p, ffn1_norm, ffn1_wgu, ffn1_wdown, mix_norm, w_in, q_norm, k_norm, sinks,
           attn_proj, dw_w, dw_b, conv_ln_g, conv_ln_b, conv_proj, w_out,
           ffn2_norm, ffn2_wgu, ffn2_wdown, ple_proj, ple_norm, ple_gate_norm, ple_gate_w, _dbg=False):
    f = lambda a: np.asarray(a, np.float32)
    x = f(x); p = f(p)[0]
    w_in = f(w_in)[0]

    def gu_layout(w):
        w = f(w)[0]
        return np.ascontiguousarray(w.reshape(D, 2, 11, 256).transpose(0, 2, 1, 3).reshape(D, 11, 512))

    oq, ok, ov, oc, og = 0, 1024, 1152, 1280, 3328
    wconv = np.ascontiguousarray(w_in[:, oc:oc + 2048].reshape(D, 2, 4, 256).transpose(0, 2, 1, 3).reshape(D, 4, 512))
    wq = np.ascontiguousarray(w_in[:, oq:oq + 1024])
    wk = w_in[:, ok:ok + 128]
    wv = w_in[:, ov:ov + 128]
    wkv = np.ascontiguousarray(np.concatenate([wk[:, 0:64], wk[:, 0:64], wk[:, 64:128], wk[:, 64:128], wv], axis=1))
    wap = f(attn_proj)[0].reshape(2, 8, 64, D).transpose(1, 0, 2, 3).reshape(D, D)
    wcp = f(conv_proj)[0]
    wga = w_in[:, og:og + 1024]
    wgb = w_in[:, og + 1024:og + 2048]
    wmerge = np.ascontiguousarray(np.stack([wap.reshape(D, 8, 128), wcp.reshape(D, 8, 128), wga.reshape(D, 8, 128), wgb.reshape(D, 8, 128)], axis=2).reshape(D, 8, 512))

    vecs = np.zeros((128, NV), np.float32)
    for off, v in ((V_FFN1, ffn1_norm), (V_MIX, mix_norm), (V_FFN2, ffn2_norm), (V_PLEN, ple_norm), (V_PLEG, ple_gate_norm),
                   (V_LNG, conv_ln_g), (V_LNB, conv_ln_b), (V_DWB, dw_b)):
        vecs[:, off:off + 8] = _col(f(v)[0])
    vecs[:, V_GQ] = np.tile(f(q_norm)[0], 2)
    vecs[:, V_GK] = np.tile(f(k_norm)[0], 2)
    vecs[0:64, V_MA] = 1.0
    vecs[64:128, V_MB] = 1.0
    dw = f(dw_w)[0]
    vecs[:, V_DW:NV] = dw.reshape(31, 8, 128).transpose(2, 1, 0).reshape(128, 8 * 31)
    sk = f(sinks)[0]
    sinks_l = np.full((128, 4), -1.0e4, np.float32)
    sinks_l[0:4, :] = sk.reshape(4, 4).T
    cmat, maskb, rind = _host_consts()

    shared = dict(wgu1=gu_layout(ffn1_wgu), wgu2=gu_layout(ffn2_wgu), wdn1=f(ffn1_wdown)[0], wdn2=f(ffn2_wdown)[0],
                  wconv=wconv, wq=wq, wkv=wkv, wmerge=wmerge, wout=f(w_out)[0], wple=f(ple_proj)[0], wpg=f(ple_gate_w)[0],
                  vecs=vecs, sinks_l=sinks_l, cmat=cmat)
    in_maps = []
    for core in range(NCORES):
        b, hf = core // 2, core % 2
        s0 = hf * TOK
        xt = np.zeros((D, HALO + TOK), np.float32)
        xt[:, HALO:] = x[b, s0:s0 + TOK].T
        if hf == 1:
            xt[:, :HALO] = x[b, s0 - HALO:s0].T
        pt = np.ascontiguousarray(p[b, s0:s0 + TOK].T)
        pos = np.arange(s0 - HALO, s0 + TOK)
        C, Sn = _rope_tables(pos)
        cs = np.ascontiguousarray(np.stack([C, Sn], axis=1))
        mask_halo = maskb if hf == 1 else np.full_like(maskb, NEG)
        cmask = np.ascontiguousarray(np.stack([maskb, mask_halo, rind], axis=1))
        d = dict(shared)
        d.update(xT=xt, pT=pt, cs=cs, cmask=cmask)
        in_maps.append(d)

    key = bool(_dbg)
    if key not in _NC_CACHE:
        _NC_CACHE[key] = build_program(dbg=_dbg)
    nc = _NC_CACHE[key]
    res = run_bass_kernel_spmd(nc, in_maps, core_ids=list(range(NCORES)))
    out = np.empty((BATCH, SEQ, D), np.float32)
    for core in range(NCORES):
        b, hf = core // 2, core % 2
        out[b, hf * TOK:(hf + 1) * TOK] = res.results[core]["outT"].T
    if _dbg:
        return out, res
    return out
```

```python
import contextlib
import numpy as np
import concourse.bass as bass
import concourse.mybir as mybir
from concourse.bass_utils import run_bass_kernel_spmd

F32 = mybir.dt.float32
BF16 = mybir.dt.bfloat16
AF = mybir.ActivationFunctionType
ALU = mybir.AluOpType

D = 1024
SEQ = 4096
BATCH = 4
NCORES = 8
TOK = 2048
HALO = 128
NT = 512
DFF = 2816
NJ = DFF // 128
EPS = 1e-6
NEG = -30000.0
NSLOT = 4
SLOT_ELEMS = 4096

V_FFN1, V_MIX, V_FFN2, V_PLEN, V_PLEG, V_LNG, V_LNB, V_DWB = [8 * i for i in range(8)]
V_GQ = 64
V_GK = 65
V_MA = 66
V_MB = 67
V_DW = 68
NV = V_DW + 8 * 31

ENGS = ("pe", "act", "dve", "pool", "sp")
CH = 30000


class Op:
    __slots__ = ("eng", "fn", "reads", "writes", "excl", "dma", "deps", "signal", "sig_no",
                 "dma_sem", "dma_use", "tag")

    def __init__(self, eng, fn, reads, writes, excl, dma):
        self.eng, self.fn, self.reads, self.writes, self.excl, self.dma = eng, fn, reads, writes, excl, dma
        self.deps = []
        self.signal = False
        self.sig_no = 0
        self.dma_sem = None
        self.dma_use = 0


class Sched:
    def __init__(self, nc, n_dma_sems=12):
        self.nc = nc
        self.ops = []
        self.last_writer = {}
        self.readers = {}
        self.n_dma_sems = n_dma_sems
        self.tag = ""

    def add(self, eng, fn, reads=(), writes=(), excl=(), dma=False):
        op = Op(eng, fn, tuple(reads), tuple(writes), tuple(excl), dma)
        op.tag = self.tag
        deps = set()
        for r in op.reads:
            w = self.last_writer.get(r)
            if w is not None:
                deps.add(w)
        for r in op.writes + op.excl:
            w = self.last_writer.get(r)
            if w is not None:
                deps.add(w)
            deps.update(self.readers.get(r, ()))
        for r in op.writes + op.excl:
            self.last_writer[r] = op
            self.readers[r] = []
        for r in op.reads:
            self.readers.setdefault(r, []).append(op)
        deps.discard(op)
        touched = set(op.reads) | set(op.writes) | set(op.excl)
        for d in deps:
            if d.eng == op.eng and not d.dma and not op.dma:
                if op.eng == "pe":
                    continue
                if op.eng != "pool" and not (set(d.writes) & touched):
                    continue
            op.deps.append(d)
            d.signal = True
        self.ops.append(op)
        return op

    def emit(self):
        nc = self.nc
        cnt = {e: 0 for e in ENGS}
        dma_cnt = {e: 0 for e in ENGS}
        for op in self.ops:
            if op.dma:
                op.dma_sem = (op.eng, dma_cnt[op.eng] % self.n_dma_sems)
                op.dma_use = dma_cnt[op.eng] // self.n_dma_sems + 1
                dma_cnt[op.eng] += 1
            elif op.signal:
                cnt[op.eng] += 1
                op.sig_no = cnt[op.eng]
        per_eng = {e: [op for op in self.ops if op.eng == e] for e in ENGS}
        with contextlib.ExitStack() as stack:
            sems = {}
            for e in ENGS:
                n = max((cnt[e] + CH - 1) // CH, 1)
                sems[e] = [stack.enter_context(nc.semaphore(f"s_{e}_{i}")) for i in range(n)]
            dsems = {}
            for e in ENGS:
                if dma_cnt[e]:
                    dsems[e] = [stack.enter_context(nc.semaphore(f"d_{e}_{i}"))
                                for i in range(min(self.n_dma_sems, dma_cnt[e]))]

            def run(eng_name, eng):
                known = {}

                def wait(key, sem, val):
                    if known.get(key, 0) >= val:
                        return
                    known[key] = val
                    eng.wait_ge(sem, val)

                for op in per_eng[eng_name]:
                    for d in op.deps:
                        if d.dma:
                            wait(("d",) + d.dma_sem, dsems[d.dma_sem[0]][d.dma_sem[1]], 16 * d.dma_use)
                        else:
                            ci = (d.sig_no - 1) // CH
                            wait((d.eng, ci), sems[d.eng][ci], (d.sig_no - 1) % CH + 1)
                    if op.fn is None:
                        continue
                    if op.dma:
                        s = dsems[op.dma_sem[0]][op.dma_sem[1]]
                        if op.dma_use > 1:
                            wait(("d",) + op.dma_sem, s, 16 * (op.dma_use - 1))
                        op.fn(eng).then_inc(s, 16)
                    else:
                        ins = op.fn(eng)
                        if op.signal:
                            ci = (op.sig_no - 1) // CH
                            ins.then_inc(sems[op.eng][ci], 1)

            with nc.Block() as block:
                @block.tensor
                def _(e):
                    run("pe", e)

                @block.scalar
                def _(e):
                    run("act", e)

                @block.vector
                def _(e):
                    run("dve", e)

                @block.gpsimd
                def _(e):
                    run("pool", e)

                @block.sync
                def _(e):
                    run("sp", e)


def build_program(dbg=False):
    nc = bass.Bass("TRN2", target_bir_lowering=False)

    def din(name, shape):
        return nc.dram_tensor(name, list(shape), F32, kind="ExternalInput").ap()

    xT = din("xT", [D, HALO + TOK])
    pT = din("pT", [256, TOK])
    wgu = [din("wgu1", [D, 11, 512]), din("wgu2", [D, 11, 512])]
    wdn = [din("wdn1", [DFF, D]), din("wdn2", [DFF, D])]
    wconv = din("wconv", [D, 4, 512])
    wq = din("wq", [D, 1024])
    wkv = din("wkv", [D, 384])
    wmerge = din("wmerge", [D, 8, 512])
    wout = din("wout", [D, D])
    wple = din("wple", [256, D])
    wpg = din("wpg", [D, D])
    vecs_d = din("vecs", [128, NV])
    sinks_d = din("sinks_l", [128, 4])
    cs_d = din("cs", [128, 2, HALO + TOK])
    cmat_d = din("cmat", [128, 4, 128])
    cmask_d = din("cmask", [128, 3, 512])
    outT = nc.dram_tensor("outT", [D, TOK], F32, kind="ExternalOutput").ap()
    dbg_out = None
    if dbg:
        dbg_out = [nc.dram_tensor(f"dbg{i}", [D, NT], F32, kind="ExternalOutput").ap() for i in range(4)]

    S = Sched(nc)
    with contextlib.ExitStack() as st:
        def sb(name, shape, dt):
            return st.enter_context(nc.sbuf_tensor(name, list(shape), dt))

        ps = [st.enter_context(nc.psum_tensor(f"ps{i}", [128, 512], F32)) for i in range(8)]
        XB = [sb("X0", [128, 8, NT], F32), sb("X1", [128, 8, NT], F32)]
        curX = [XB[0], 0]
        H = sb("H", [128, 8, NT], BF16)
        R1 = sb("R1", [128, NJ + 2, NT], BF16)
        Kb = [sb(f"Kb{i}", [128, 2, 640], BF16) for i in range(2)]
        Vb = [sb(f"Vb{i}", [128, 4, 128], BF16) for i in range(2)]
        U = sb("U", [128, 8, 30 + NT], BF16)
        DG = [sb(f"DG{i}", [128, 31, 128], BF16) for i in range(2)]
        Y = sb("Y", [128, 8, NT], F32)
        ER = [sb(f"ER{i}", [128, 8, 256], BF16) for i in range(3)]
        CN = sb("CN", [128, 8, NT], BF16)
        CS = sb("CS", [128, 2, NT], F32)
        PT = sb("PT", [128, 2, NT], BF16)
        WS = [sb(f"WS{i}", [128, SLOT_ELEMS], BF16) for i in range(NSLOT)]
        vecs = sb("vecs_sb", [128, NV], F32)
        sinks_t = sb("sinks_t", [128, 4], F32)
        esc = sb("esc", [128, 4], F32)
        cmat = sb("cmat_sb", [128, 4, 128], BF16)
        cmask = sb("cmask_sb", [128, 3, 512], BF16)
        Lm = sb("Lm", [128, 4, 128], BF16)
        EPSC = sb("EPSC", [128, 1], F32)
        ident, ones, blkones, perm = (cmat[:, i, :] for i in range(4))

        ACTB = lambda j: R1[:, j, :]
        QA = lambda c: R1[:, c, :]
        QB = lambda c: R1[:, 8 + c, :]

        def ATv(p0, p1, j0, j1, q0, q1):
            return R1[p0:p1, 16 + j0:16 + j1, q0:q1]

        tmps = {}
        SHARED = {"SG": "F", "QR": "F", "TB": "F", "RD": "F", "T1": "F", "TGA": "F", "TGB": "F", "M1": "F", "M2": "F",
                  "VE": "F", "MSQ": "F", "SQ": "B", "A": "B", "AC": "B", "AS": "B", "YB": "B", "YS": "B"}
        RING_N = {"F": 8, "B": 6, "RS": 2, "RSQ": 2, "NMU": 1}

        def tmp(name, dt, n=2):
            rn = SHARED.get(name, name)
            if rn not in tmps:
                rdt = F32 if rn != "B" else BF16
                assert rdt == dt, (name, dt)
                tmps[rn] = [[sb(f"t_{rn}{i}", [128, NT], rdt) for i in range(RING_N[rn])], 0]
            lst = tmps[rn]
            i = lst[1] % len(lst[0])
            lst[1] += 1
            return lst[0][i], ("T", rn, i)

        bank_ctr = [0]

        def nb(excluded=()):
            while True:
                b = bank_ctr[0] % 8
                bank_ctr[0] += 1
                if b not in excluded:
                    return b

        P = lambda b: ("ps", b)

        wblocks = []
        wstate = {"emitted": 0, "next": 0}

        def plan_blocks():
            def tile_blocks(halo):
                bl = []

                def ffn(i):
                    for jb in range(11):
                        if i == 1 and jb == 4:
                            bl.append(("ple", wple.rearrange("(kc p) c -> p kc c", p=128), [2, 1024]))
                        bl.append(("gu", wgu[i].rearrange("(kc p) j c -> p kc j c", p=128)[:, :, jb, :], [8, 512]))
                    for jb in range(11):
                        bl.append(("dn", wdn[i].rearrange("(j p) m -> p j m", p=128)[:, 2 * jb:2 * jb + 2, :], [2, 1024]))
                ffn(0)
                for cb in range(4):
                    bl.append(("conv", wconv.rearrange("(kc p) j c -> p kc j c", p=128)[:, :, cb, :], [8, 512]))
                if not halo:
                    for qb in range(2):
                        bl.append(("q", wq.rearrange("(kc p) c -> p kc c", p=128)[:, :, qb * 512:(qb + 1) * 512], [8, 512]))
                bl.append(("kv", wkv.rearrange("(kc p) c -> p kc c", p=128), [8, 384]))
                if not halo:
                    for c in range(8):
                        bl.append(("mg", wmerge.rearrange("(kc p) j c -> p kc j c", p=128)[:, :, c, :], [8, 512]))
                    for ob in range(2):
                        bl.append(("wo", wout.rearrange("(kc p) c -> p kc c", p=128)[:, :, ob * 512:(ob + 1) * 512], [8, 512]))
                    ffn(1)
                    for ob in range(2):
                        bl.append(("pg", wpg.rearrange("(kc p) c -> p kc c", p=128)[:, :, ob * 512:(ob + 1) * 512], [8, 512]))
                return bl
            wblocks.extend(tile_blocks(True))
            for _ in range(TOK // NT):
                wblocks.extend(tile_blocks(False))

        plan_blocks()

        def wview(i):
            kind, src, (a, b) = wblocks[i]
            return WS[i % NSLOT][:, 0:a * b].rearrange("p (a b) -> p a b", a=a)

        def wnext(kind):
            i = wstate["next"]
            assert wblocks[i][0] == kind, (wblocks[i][0], kind, i)
            wstate["next"] += 1
            while wstate["emitted"] < min(len(wblocks), i + NSLOT):
                j = wstate["emitted"]
                dst = wview(j)
                src = wblocks[j][1]
                S.add("pool", (lambda d, s_: lambda e: e.dma_start(out=d, in_=s_))(dst, src),
                      writes=[("W", j % NSLOT)], dma=True)
                wstate["emitted"] += 1
            return wview(i), ("W", i % NSLOT)

        S.add("sp", lambda e: e.dma_start(out=vecs[:], in_=vecs_d), writes=["vecs"], dma=True)
        S.add("sp", lambda e: e.dma_start(out=sinks_t[:], in_=sinks_d), writes=["sinks_t"], dma=True)
        S.add("pool", lambda e: e.dma_start(out=cmat[:], in_=cmat_d), writes=["cmat"], dma=True)
        S.add("pool", lambda e: e.dma_start(out=cmask[:], in_=cmask_d), writes=["cmask"], dma=True)
        S.add("dve", lambda e: e.memset(EPSC[:], EPS), writes=["EPSC"])
        S.add("act", lambda e: e.activation(out=esc[:], in_=sinks_t[:], func=AF.Exp), reads=["sinks_t"], writes=["esc"])
        for m in range(4):
            S.add("dve", (lambda m: lambda e: e.tensor_scalar(out=Lm[:, m, :], in0=ones, scalar1=esc[:, m:m + 1], scalar2=None, op0=ALU.mult))(m),
                  reads=["esc", "cmat"], writes=[("Lm", m)])
        S.add("dve", lambda e: e.tensor_scalar(out=vecs[:, V_DW:NV], in0=vecs[:, V_DW:NV], scalar1=0.5, scalar2=None, op0=ALU.mult),
              reads=["vecs"], writes=["vecs"])


        def MM(out, lhsT, rhs, start, stop, skip=False):
            return lambda e: e.matmul(out, lhsT=lhsT, rhs=rhs, start=start, stop=stop, skip_group_check=skip)

        def ACTF(out, in_, func, scale=None, bias=None):
            kw = {}
            if scale is not None:
                kw["scale"] = scale
            if bias is not None:
                kw["bias"] = bias
            return lambda e: e.activation(out=out, in_=in_, func=func, **kw)

        def TT(out, in0, in1, op):
            return lambda e: e.tensor_tensor(out=out, in0=in0, in1=in1, op=op)

        def STT(out, in0, scalar, in1, op0, op1):
            return lambda e: e.scalar_tensor_tensor(out=out, in0=in0, scalar=scalar, in1=in1, op0=op0, op1=op1)

        def TS(out, in0, s1, s2, op0, op1=None):
            if op1 is None:
                return lambda e: e.tensor_scalar(out=out, in0=in0, scalar1=s1, scalar2=None, op0=op0)
            return lambda e: e.tensor_scalar(out=out, in0=in0, scalar1=s1, scalar2=s2, op0=op0, op1=op1)

        def DMA(out, in_):
            return lambda e: e.dma_start(out=out, in_=in_)

        R1k = lambda i: ("R1", i)
        Xk = lambda kc: ("X", curX[1], kc)
        Hk = lambda kc: ("H", kc)

        def rstd_from_psum(bank, n, inv_d, ring="RS"):
            LN_, kLN = tmp("VE", F32)
            RS, kRS = tmp(ring, F32)
            S.add("act", ACTF(LN_[:, 0:n], ps[bank][:, 0:n], AF.Ln, scale=inv_d, bias=EPSC[:, 0:1]), excl=[P(bank)], reads=["EPSC"], writes=[kLN])
            S.add("act", ACTF(RS[:, 0:n], LN_[:, 0:n], AF.Exp, scale=-0.5), reads=[kLN], writes=[kRS])
            return RS, kRS

        def rmsnorm_to_H(n, vcol):
            b = nb()
            for kc in range(8):
                SQ, kSQ = tmp("SQ", BF16, 3)
                S.add("act", ACTF(SQ[:, 0:n], curX[0][:, kc, 0:n], AF.Square), reads=[Xk(kc)], writes=[kSQ])
                S.add("pe", MM(ps[b][:, 0:n], ones, SQ[:, 0:n], kc == 0, kc == 7), reads=[kSQ, "cmat"], excl=[P(b)])
            RS, kRS = rstd_from_psum(b, n, 1.0 / D)
            for kc in range(8):
                S.add("dve", STT(H[:, kc, 0:n], curX[0][:, kc, 0:n], vecs[:, vcol + kc:vcol + kc + 1], RS[:, 0:n], ALU.mult, ALU.mult),
                      reads=[Xk(kc), kRS, "vecs"], writes=[Hk(kc)])

        def proj8(W, kW, col0, rhs_fn, rhs_key_fn, n, bank=None, extra_reads=()):
            b = nb() if bank is None else bank
            for kc in range(8):
                S.add("pe", MM(ps[b][:, 0:n], W[:, kc, col0:col0 + 128], rhs_fn(kc), kc == 0, kc == 7),
                      reads=[kW, rhs_key_fn(kc)], excl=[P(b)])
            return b

        def ffn(n, vcol, tg='', mid_hook=None):
            S.tag = tg + '.norm'
            X_ = curX[0]
            for kc in range(8):
                S.add("dve", TS(H[:, kc, 0:n], X_[:, kc, 0:n], vecs[:, vcol + kc:vcol + kc + 1], None, ALU.mult),
                      reads=[Xk(kc), "vecs"], writes=[Hk(kc)])
            bst = nb()
            for kc in range(8):
                SQ, kSQ = tmp("SQ", BF16, 3)
                S.add("act", ACTF(SQ[:, 0:n], X_[:, kc, 0:n], AF.Square), reads=[Xk(kc)], writes=[kSQ])
                S.add("pe", MM(ps[bst][:, 0:n], ones, SQ[:, 0:n], kc == 0, kc == 7), reads=[kSQ, "cmat"], excl=[P(bst)])
            RS, kRS = rstd_from_psum(bst, n, 1.0 / D)
            S.tag = tg + '.gu'
            Hr = lambda kc: H[:, kc, 0:n]
            for jb in range(11):
                if mid_hook is not None and jb == 4:
                    mid_hook()
                    S.tag = tg + '.gu'
                W, kW = wnext("gu")
                for jj in range(2):
                    j = 2 * jb + jj
                    bg = proj8(W, kW, jj * 128, Hr, Hk, n)
                    bu = proj8(W, kW, 256 + jj * 128, Hr, Hk, n)
                    G1, kG1 = tmp("T1", F32)
                    SG, kSG = tmp("SG", F32)
                    S.add("dve", TT(G1[:, 0:n], ps[bg][:, 0:n], RS[:, 0:n], ALU.mult), reads=[kRS], excl=[P(bg)], writes=[kG1])
                    S.add("act", ACTF(SG[:, 0:n], G1[:, 0:n], AF.Silu), reads=[kG1], writes=[kSG])
                    S.add("dve", TT(R1[:, j, 0:n], SG[:, 0:n], ps[bu][:, 0:n], ALU.mult), reads=[kSG], excl=[P(bu)], writes=[R1k(j)])
            S.tag = tg + '.down'
            for jb in range(11):
                W, kW = wnext("dn")
                for jj in range(2):
                    j = 2 * jb + jj
                    for m in range(8):
                        S.add("pe", MM(ps[m][:, 0:n], W[:, jj, m * 128:(m + 1) * 128], R1[:, j, 0:n], j == 0, j == NJ - 1),
                              reads=[kW, R1k(j)], excl=[P(m)])
            for m in range(8):
                DT, kDT = tmp("M1", F32)
                S.add("dve", STT(DT[:, 0:n], ps[m][:, 0:n], 0.5, RS[:, 0:n], ALU.mult, ALU.mult), reads=[kRS], excl=[P(m)], writes=[kDT])
                S.add("pool", TT(X_[:, m, 0:n], DT[:, 0:n], X_[:, m, 0:n], ALU.add), reads=[kDT], writes=[Xk(m)])

        def qk_pipeline(items, n):
            st_ = {}

            def s1(i):
                proj_fn, vg, dst_fn = items[i]
                bank = proj_fn()
                SQ, kSQ = tmp("SQ", BF16)
                S.add("act", ACTF(SQ[:, 0:n], ps[bank][:, 0:n], AF.Square), excl=[P(bank)], writes=[kSQ])
                st_[i] = (bank, SQ, kSQ)

            def s2(i):
                proj_fn, vg, dst_fn = items[i]
                bank, SQ, kSQ = st_[i]
                b2 = nb()
                S.add("pe", MM(ps[b2][:, 0:n], blkones, SQ[:, 0:n], True, True), reads=[kSQ, "cmat"], excl=[P(b2)])
                RS, kRS = rstd_from_psum(b2, n, 1.0 / 64, ring="RSQ")
                A, kA = tmp("A", BF16)
                S.add("dve", STT(A[:, 0:n], ps[bank][:, 0:n], vecs[:, vg:vg + 1], RS[:, 0:n], ALU.mult, ALU.mult), reads=[kRS, "vecs"], excl=[P(bank)], writes=[kA])
                AC, kAC = tmp("AC", BF16)
                AS, kAS = tmp("AS", BF16)
                S.add("dve", TT(AC[:, 0:n], A[:, 0:n], CS[:, 0, 0:n], ALU.mult), reads=[kA, "CS"], writes=[kAC])
                S.add("pool", TT(AS[:, 0:n], A[:, 0:n], CS[:, 1, 0:n], ALU.mult), reads=[kA, "CS"], writes=[kAS])
                st_[i] = (AC, kAC, AS, kAS)

            def s3(i):
                proj_fn, vg, dst_fn = items[i]
                AC, kAC, AS, kAS = st_[i]
                b3 = nb()
                S.add("pe", MM(ps[b3][:, 0:n], ident, AC[:, 0:n], True, False), reads=[kAC, "cmat"], excl=[P(b3)])
                S.add("pe", MM(ps[b3][:, 0:n], perm, AS[:, 0:n], False, True), reads=[kAS, "cmat"], excl=[P(b3)])
                dst_fn(b3)

            ni = len(items)
            for step in range(ni + 2):
                if step < ni:
                    s1(step)
                if 0 <= step - 1 < ni:
                    s2(step - 1)
                if 0 <= step - 2 < ni:
                    s3(step - 2)

        def tile(ti):
            halo = ti == 0
            n = HALO if halo else NT
            c0 = 0 if halo else HALO + (ti - 1) * NT
            par = ti % 2
            Hr = lambda kc: H[:, kc, 0:n]
            curX[0], curX[1] = XB[par], par
            allX = [Xk(kc) for kc in range(8)]
            S.tag = f'{ti}.load'

            def emit_xload(tj):
                nj = HALO if tj == 0 else NT
                cj = 0 if tj == 0 else HALO + (tj - 1) * NT
                pj = tj % 2
                S.add("sp", DMA(XB[pj][:, :, 0:nj], xT.rearrange("(kc p) n -> p kc n", p=128)[:, :, cj:cj + nj]),
                      writes=[("X", pj, kc) for kc in range(8)], dma=True)
            if ti == 0:
                emit_xload(0)
            S.add("sp", DMA(CS[:, :, 0:n], cs_d[:, :, c0:c0 + n]), writes=["CS"], dma=True)
            if not halo:
                S.add("pool", DMA(PT[:], pT.rearrange("(kc p) n -> p kc n", p=128)[:, :, c0 - HALO:c0 - HALO + NT]), writes=["PT"], dma=True)
            ffn(n, V_FFN1, f'{ti}.ffn1')
            if ti + 1 <= TOK // NT:
                S.tag = f'{ti}.load'
                emit_xload(ti + 1)
            if dbg and ti == 1:
                S.add("sp", DMA(dbg_out[0].rearrange("(kc p) n -> p kc n", p=128), curX[0][:]), reads=allX, writes=["dbg0"], dma=True)
            S.tag = f'{ti}.mixnorm'
            rmsnorm_to_H(n, V_MIX)
            S.tag = f'{ti}.convproj'
            for cb in range(4):
                W, kW = wnext("conv")
                for cc in range(2):
                    c = 2 * cb + cc
                    ba = proj8(W, kW, cc * 128, Hr, Hk, n)
                    bb = proj8(W, kW, 256 + cc * 128, Hr, Hk, n)
                    TB, kTB = tmp("TB", F32)
                    if halo:
                        S.add("act", ACTF(TB[:, 0:30], ps[bb][:, n - 30:n], AF.Tanh, scale=0.5), excl=[P(bb)], writes=[kTB])
                        S.add("dve", STT(U[:, c, 0:30], TB[:, 0:30], 1.0, ps[ba][:, n - 30:n], ALU.add, ALU.mult), reads=[kTB], excl=[P(ba)], writes=[("U", c)])
                    else:
                        S.add("act", ACTF(TB[:, 0:n], ps[bb][:, 0:n], AF.Tanh, scale=0.5), excl=[P(bb)], writes=[kTB])
                        S.add("dve", STT(U[:, c, 30:30 + n], TB[:, 0:n], 1.0, ps[ba][:, 0:n], ALU.add, ALU.mult), reads=[kTB], excl=[P(ba)], writes=[("U", c)])
            S.tag = f'{ti}.q'
            if not halo:
                pass
            items = []
            wq_state = {}

            def mk_q(c):
                def proj():
                    qb_, cc = divmod(c, 4)
                    if cc == 0:
                        wq_state["W"] = wnext("q")
                    W, kW = wq_state["W"]
                    return proj8(W, kW, cc * 128, Hr, Hk, n)

                def qdst(b3):
                    S.add("act", ACTF(R1[:, c, 0:n], ps[b3][:, 0:n], AF.Copy, scale=vecs[:, V_MA:V_MA + 1]), reads=["vecs"], excl=[P(b3)], writes=[R1k(c)])
                    S.add("dve", TS(R1[:, 8 + c, 0:n], ps[b3][:, 0:n], vecs[:, V_MB:V_MB + 1], None, ALU.mult), reads=["vecs"], excl=[P(b3)], writes=[R1k(8 + c)])
                return (proj, V_GQ, qdst)

            kcol0 = 512 if halo else 128

            def mk_k(g):
                def proj():
                    if g == 0:
                        wq_state["KV"] = wnext("kv")
                    W, kW = wq_state["KV"]
                    return proj8(W, kW, g * 128, Hr, Hk, n)

                def kdst(b3):
                    S.add("act", ACTF(Kb[par][:, g, kcol0:kcol0 + n], ps[b3][:, 0:n], AF.Copy), excl=[P(b3)], writes=[("K", par, g)])
                return (proj, V_GK, kdst)

            if not halo:
                items += [mk_q(c) for c in range(8)]
            items += [mk_k(g) for g in range(2)]
            qk_pipeline(items, n)
            S.tag = f'{ti}.kv'
            W, kW = wq_state["KV"]
            for b in range(n // 128):
                bv = nb()
                vslot = 3 if halo else b
                for kc in range(8):
                    S.add("pe", MM(ps[bv][:, 0:128], H[:, kc, b * 128:(b + 1) * 128], W[:, kc, 256:384], kc == 0, kc == 7),
                          reads=[kW, Hk(kc)], excl=[P(bv)])
                S.add("act", ACTF(Vb[par][:, vslot, :], ps[bv][:, 0:128], AF.Copy), excl=[P(bv)], writes=[("V", par, vslot)])
            if halo:
                return
            S.tag = f'{ti}.conv'
            for c in range(8):
                wc = V_DW + c * 31
                dg = DG[c % 2]
                S.add("dve", TT(dg[:], ident.unsqueeze(1).broadcast_to([128, 31, 128]),
                                 vecs[:, wc:wc + 31].unsqueeze(2).broadcast_to([128, 31, 128]), ALU.mult),
                      reads=["cmat", "vecs"], writes=[("DG", c % 2)])
                bcv = nb()
                for j in range(31):
                    S.add("pe", MM(ps[bcv][:, :], dg[:, j, :], U[:, c, j:j + NT], j == 0, j == 30), reads=[("DG", c % 2), ("U", c)], excl=[P(bcv)])
                S.add("act", ACTF(Y[:, c, :], ps[bcv][:, :], AF.Identity, bias=vecs[:, V_DWB + c:V_DWB + c + 1]), reads=["vecs"], excl=[P(bcv)], writes=[("Y", c)])
                S.add("pool", (lambda c: lambda e: e.tensor_copy(out=U[:, c, 0:30], in_=U[:, c, NT:NT + 30]))(c), writes=[("U", c)])
            S.tag = f'{ti}.ln'
            b1 = nb()
            b2 = nb()
            for c in range(8):
                YB, kYB = tmp("YB", BF16)
                YS, kYS = tmp("YS", BF16)
                S.add("act", ACTF(YB[:], Y[:, c, :], AF.Copy), reads=[("Y", c)], writes=[kYB])
                S.add("act", ACTF(YS[:], Y[:, c, :], AF.Square), reads=[("Y", c)], writes=[kYS])
                S.add("pe", MM(ps[b1][:, :], ones, YB[:], c == 0, c == 7), reads=[kYB, "cmat"], excl=[P(b1)])
                S.add("pe", MM(ps[b2][:, :], ones, YS[:], c == 0, c == 7), reads=[kYS, "cmat"], excl=[P(b2)])
            NMU, kNMU = tmp("NMU", F32, 1)
            MSQ, kMSQ = tmp("MSQ", F32, 1)
            VE, kVE = tmp("VE", F32)
            RS, kRS = tmp("RS", F32)
            S.add("dve", TS(NMU[:], ps[b1][:, :], -1.0 / D, None, ALU.mult), excl=[P(b1)], writes=[kNMU])
            S.add("dve", TT(MSQ[:], NMU[:], NMU[:], ALU.mult), reads=[kNMU], writes=[kMSQ])
            S.add("dve", STT(VE[:], ps[b2][:, :], 1.0 / D, MSQ[:], ALU.mult, ALU.subtract), reads=[kMSQ], excl=[P(b2)], writes=[kVE])
            S.add("act", ACTF(VE[:], VE[:], AF.Ln, scale=1.0, bias=EPSC[:, 0:1]), reads=[kVE, "EPSC"], writes=[kVE])
            S.add("act", ACTF(RS[:], VE[:], AF.Exp, scale=-0.5), reads=[kVE], writes=[kRS])
            def ln_normalize():
                S.tag = f'{ti}.ln2'
                for c in range(8):
                    T1, kT1 = tmp("T1", F32)
                    S.add("dve", TT(T1[:], Y[:, c, :], NMU[:], ALU.add), reads=[("Y", c), kNMU], writes=[kT1])
                    S.add("dve", TT(T1[:], T1[:], RS[:], ALU.mult), reads=[kT1, kRS], writes=[kT1])
                    S.add("act", ACTF(CN[:, c, :], T1[:], AF.Silu, scale=vecs[:, V_LNG + c:V_LNG + c + 1], bias=vecs[:, V_LNB + c:V_LNB + c + 1]),
                          reads=[kT1, "vecs"], writes=[("CN", c)])
                S.tag = f'{ti}.attn'
            S.tag = f'{ti}.attn'
            ecnt = [0]
            for g in range(2):
                Eprev = None
                if g == 1:
                    ln_normalize()
                for kbl in range(-1, 4):
                    q_lo = max(kbl, 0) * 128
                    q_hi = min(kbl + 2, 4) * 128
                    ncols = q_hi - q_lo
                    rel0 = q_lo - kbl * 128
                    if kbl == -1:
                        kT = Kb[1 - par][:, g, 512:640]
                        kkey = ("K", 1 - par, g)
                    else:
                        kT = Kb[par][:, g, 128 + kbl * 128:256 + kbl * 128]
                        kkey = ("K", par, g)
                    mi = 1 if (kbl == -1 and ti == 1) else 0
                    er = ecnt[0] % 3
                    ecnt[0] += 1
                    Et = ER[er]
                    for pp in range(4):
                        c = 4 * g + pp
                        b = nb()
                        pv = ps[b][:, :].rearrange("p (h q) -> p h q", h=2)[:, :, rel0:rel0 + ncols]
                        mv = cmask[:, mi, :].rearrange("p (h q) -> p h q", h=2)[:, :, rel0:rel0 + ncols]
                        if ncols == 256:
                            S.add("pe", MM(ps[b][:, :], ident, cmask[:, mi, :], True, False, skip=True), reads=["cmat", "cmask"], excl=[P(b)])
                        else:
                            S.add("pe", MM(ps[b][:, rel0:rel0 + ncols], ident, cmask[:, mi, rel0:rel0 + ncols], True, False, skip=True),
                                  reads=["cmat", "cmask"], excl=[P(b)])
                            S.add("pe", MM(ps[b][:, 256 + rel0:256 + rel0 + ncols], ident, cmask[:, mi, 256 + rel0:256 + rel0 + ncols], False, False, skip=True),
                                  reads=["cmat", "cmask"], excl=[P(b)])
                        S.add("pe", MM(ps[b][:, rel0:rel0 + ncols], kT, R1[:, c, q_lo:q_hi], False, False, skip=True),
                              reads=[kkey, R1k(c)], excl=[P(b)])
                        S.add("pe", MM(ps[b][:, 256 + rel0:256 + rel0 + ncols], kT, R1[:, 8 + c, q_lo:q_hi], False, True, skip=True),
                              reads=[kkey, R1k(8 + c)], excl=[P(b)])
                        S.add("act", ACTF(Et[:, 2 * pp:2 * pp + 2, rel0:rel0 + ncols], pv, AF.Exp, scale=0.125), excl=[P(b)], writes=[("E", er, pp)])
                    if kbl >= 0:
                        qb = kbl
                        if kbl == 0:
                            Vprev, kVp = Vb[1 - par][:, 3, :], ("V", 1 - par, 3)
                        else:
                            Vprev, kVp = Vb[par][:, kbl - 1, :], ("V", par, kbl - 1)
                        Vcur, kVc = Vb[par][:, kbl, :], ("V", par, kbl)
                        erp = Eprev
                        for half in range(2):
                            bn_ = nb()
                            bd_ = nb()
                            ep = ER[erp][:, 4 * half:4 * half + 4, 128:256]
                            ec = Et[:, 4 * half:4 * half + 4, 0:128]
                            ekeys = [("E", erp, 2 * half), ("E", erp, 2 * half + 1), ("E", er, 2 * half), ("E", er, 2 * half + 1)]
                            pn = ps[bn_][:, :].rearrange("p (h q) -> p h q", h=4)
                            pd = ps[bd_][:, :].rearrange("p (h q) -> p h q", h=4)
                            S.add("pe", MM(ps[bn_][:, :], Vprev, ep, True, False), reads=[kVp] + ekeys, excl=[P(bn_)])
                            S.add("pe", MM(ps[bn_][:, :], Vcur, ec, False, True), reads=[kVc] + ekeys, excl=[P(bn_)])
                            S.add("pe", MM(ps[bd_][:, :], ones, ep, True, False), reads=["cmat"] + ekeys, excl=[P(bd_)])
                            S.add("pe", MM(ps[bd_][:, :], ones, ec, False, False), reads=["cmat"] + ekeys, excl=[P(bd_)])
                            mm = 2 * g + half
                            S.add("pe", MM(ps[bd_][:, :], Lm[:, mm, :], cmask[:, 2, :], False, True), reads=[("Lm", mm), "cmask"], excl=[P(bd_)])
                            RD, kRD = tmp("RD", F32)
                            r0, r1 = g * 64, (g + 1) * 64
                            S.add("act", ACTF(RD[r0:r1, :], ps[bd_][r0:r1, :], AF.Ln), excl=[P(bd_)], writes=[kRD])
                            S.add("act", ACTF(RD[r0:r1, :], RD[r0:r1, :], AF.Exp, scale=-1.0), reads=[kRD], writes=[kRD])
                            S.add("dve", TT(R1[r0:r1, 16 + 4 * half:16 + 4 * half + 4, qb * 128:(qb + 1) * 128], pn[r0:r1, :, :],
                                            RD[r0:r1, :].rearrange("p (h q) -> p h q", h=4), ALU.mult),
                                  reads=[kRD], excl=[P(bn_)], writes=[R1k(16 + 4 * half + jj) for jj in range(4)])
                    Eprev = er
            S.tag = f'{ti}.merge'
            for c in range(8):
                W, kW = wnext("mg")
                b_ga = proj8(W, kW, 256, Hr, Hk, NT)
                b_gb = proj8(W, kW, 384, Hr, Hk, NT)
                b_c = proj8(W, kW, 128, lambda kc: CN[:, kc, :], lambda kc: ("CN", kc), NT)
                b_a = proj8(W, kW, 0, lambda kc: R1[:, 16 + kc, :], lambda kc: R1k(16 + kc), NT)
                TGA, kTGA = tmp("TGA", F32)
                TGB, kTGB = tmp("TGB", F32)
                M1, kM1 = tmp("M1", F32)
                M2, kM2 = tmp("M2", F32)
                S.add("act", ACTF(TGA[:], ps[b_ga][:, :], AF.Tanh, scale=0.5), excl=[P(b_ga)], writes=[kTGA])
                S.add("act", ACTF(TGB[:], ps[b_gb][:, :], AF.Tanh, scale=0.5), excl=[P(b_gb)], writes=[kTGB])
                S.add("dve", STT(M1[:], TGA[:], 1.0, ps[b_a][:, :], ALU.add, ALU.mult), reads=[kTGA], excl=[P(b_a)], writes=[kM1])
                S.add("dve", STT(M2[:], TGB[:], 1.0, ps[b_c][:, :], ALU.add, ALU.mult), reads=[kTGB], excl=[P(b_c)], writes=[kM2])
                S.add("pool", TT(R1[:, c, :], M1[:], M2[:], ALU.add), reads=[kM1, kM2], writes=[R1k(c)])
            S.tag = f'{ti}.wout'
            for ob in range(2):
                W, kW = wnext("wo")
                for mm in range(4):
                    m = 4 * ob + mm
                    b = proj8(W, kW, mm * 128, lambda kc: R1[:, kc, :], lambda kc: R1k(kc), NT)
                    S.add("dve", STT(curX[0][:, m, :], ps[b][:, :], 0.5, curX[0][:, m, :], ALU.mult, ALU.add), excl=[P(b)], writes=[Xk(m)])
            if dbg and ti == 1:
                S.add("sp", DMA(dbg_out[1].rearrange("(kc p) n -> p kc n", p=128), curX[0][:]), reads=allX, writes=["dbg1"], dma=True)
            rse_box = {}

            def ple_e_part():
                S.tag = f'{ti}.ple_e'
                W, kW = wnext("ple")
                bss = nb()
                pend = []
                for m in range(8):
                    b = nb(excluded=(bss,))
                    for kc in range(2):
                        S.add("pe", MM(ps[b][:, :], W[:, kc, m * 128:(m + 1) * 128], PT[:, kc, :], kc == 0, kc == 1), reads=[kW, "PT"], excl=[P(b)])
                    SQ, kSQ = tmp("SQ", BF16, 3)
                    S.add("act", ACTF(Y[:, m, :], ps[b][:, :], AF.Copy), excl=[P(b)], writes=[("Y", m)])
                    S.add("act", ACTF(SQ[:], ps[b][:, :], AF.Square), excl=[P(b)], writes=[kSQ])
                    pend.append((SQ, kSQ, m))
                    if len(pend) > 2:
                        SQ_, kSQ_, m_ = pend.pop(0)
                        S.add("pe", MM(ps[bss][:, :], ones, SQ_[:], m_ == 0, m_ == 7), reads=[kSQ_, "cmat"], excl=[P(bss)])
                for SQ_, kSQ_, m_ in pend:
                    S.add("pe", MM(ps[bss][:, :], ones, SQ_[:], m_ == 0, m_ == 7), reads=[kSQ_, "cmat"], excl=[P(bss)])
                RSe, kRSe = rstd_from_psum(bss, NT, 1.0 / D)
                rse_box['v'] = (RSe, kRSe)
                for m in range(8):
                    S.add("dve", STT(Y[:, m, :], Y[:, m, :], vecs[:, V_PLEN + m:V_PLEN + m + 1], RSe[:], ALU.mult, ALU.mult),
                          reads=[kRSe, "vecs"], writes=[("Y", m)])
            ffn(n, V_FFN2, f'{ti}.ffn2', mid_hook=ple_e_part)
            if dbg and ti == 1:
                S.add("sp", DMA(dbg_out[2].rearrange("(kc p) n -> p kc n", p=128), curX[0][:]), reads=allX, writes=["dbg2"], dma=True)
            S.tag = f'{ti}.ple'
            rmsnorm_to_H(NT, V_PLEG)
            for ob in range(2):
                W, kW = wnext("pg")
                for mm in range(4):
                    m = 4 * ob + mm
                    b = proj8(W, kW, mm * 128, Hr, Hk, NT)
                    TG, kTG = tmp("TGA", F32)
                    E1, kE1 = tmp("M1", F32)
                    S.add("act", ACTF(TG[:], ps[b][:, :], AF.Tanh, scale=0.5), excl=[P(b)], writes=[kTG])
                    S.add("dve", STT(E1[:], TG[:], 1.0, Y[:, m, :], ALU.add, ALU.mult), reads=[kTG, ("Y", m)], writes=[kE1])
                    S.add("dve", STT(curX[0][:, m, :], E1[:], 0.5, curX[0][:, m, :], ALU.mult, ALU.add), reads=[kE1], writes=[Xk(m)])
            S.tag = f'{ti}.store'
            o0 = c0 - HALO
            S.add("sp", DMA(outT.rearrange("(kc p) n -> p kc n", p=128)[:, :, o0:o0 + NT], curX[0][:]), reads=allX, writes=[("out", ti)], dma=True)

        for ti in range(1 + TOK // NT):
            tile(ti)
        S.add("sp", None, reads=[("out", ti) for ti in range(1, 1 + TOK // NT)] + (["dbg0", "dbg1", "dbg2"] if dbg else []))
        assert wstate["next"] == len(wblocks), (wstate, len(wblocks))
        print('sbuf bytes remaining', nc.sbuf_bytes_remaining)
        S.emit()
    global _LAST_S
    _LAST_S = S
    return nc


def _col(v):
    return np.ascontiguousarray(np.asarray(v, np.float32).reshape(8, 128).T)


def _host_consts():
    ident = np.eye(128, dtype=np.float32)
    ones = np.ones((128, 128), np.float32)
    blk = np.zeros((128, 128), np.float32)
    blk[0:64, 0:64] = 1
    blk[64:128, 64:128] = 1
    perm = np.zeros((128, 128), np.float32)
    for h in range(2):
        for d in range(8):
            perm[h * 64 + d + 8, h * 64 + d] = -1.0
            perm[h * 64 + d, h * 64 + d + 8] = 1.0
    cmat = np.stack([ident, ones, blk, perm], axis=1)
    k = np.arange(128)[:, None]
    q = np.arange(256)[None, :]
    delta = q - k
    m = np.where((delta >= 0) & (delta < 128), 0.0, NEG).astype(np.float32)
    maskb = np.concatenate([m, m], axis=1)
    rind = np.zeros((128, 512), np.float32)
    for r in range(4):
        rind[r, r * 128:(r + 1) * 128] = 1.0
    return cmat, maskb, rind


def _rope_tables(pos):
    inv_freq = 500000.0 ** (-(np.arange(0, 16, 2, dtype=np.float64)) / 16.0)
    ang = pos.astype(np.float64)[None, :] * inv_freq[:, None]
    cos = np.cos(ang).astype(np.float32)
    sin = np.sin(ang).astype(np.float32)
    C = np.ones((128, pos.shape[0]), np.float32)
    Sn = np.zeros((128, pos.shape[0]), np.float32)
    for h in range(2):
        for d in range(16):
            C[h * 64 + d] = cos[d % 8]
            Sn[h * 64 + d] = sin[d % 8]
    return C, Sn


_NC_CACHE = {}
_LAST_S = None


def kernel(x, p, ffn1_norm, ffn1_wgu, ffn1_wdown, mix_norm, w_in, q_norm, k_norm, sinks,
           attn_proj, dw_w, dw_b, conv_ln_g, conv_ln_b, conv_proj, w_out,
           ffn2_norm, ffn2_wgu, ffn2_wdown, ple_proj, ple_norm, ple_gate_norm, ple_gate_w, _dbg=False):
    f = lambda a: np.asarray(a, np.float32)
    x = f(x); p = f(p)[0]
    w_in = f(w_in)[0]

    def gu_layout(w):
        w = f(w)[0]
        return np.ascontiguousarray(w.reshape(D, 2, 11, 256).transpose(0, 2, 1, 3).reshape(D, 11, 512))

    oq, ok, ov, oc, og = 0, 1024, 1152, 1280, 3328
    wconv = np.ascontiguousarray(w_in[:, oc:oc + 2048].reshape(D, 2, 4, 256).transpose(0, 2, 1, 3).reshape(D, 4, 512))
    wq = np.ascontiguousarray(w_in[:, oq:oq + 1024])
    wk = w_in[:, ok:ok + 128]
    wv = w_in[:, ov:ov + 128]
    wkv = np.ascontiguousarray(np.concatenate([wk[:, 0:64], wk[:, 0:64], wk[:, 64:128], wk[:, 64:128], wv], axis=1))
    wap = f(attn_proj)[0].reshape(2, 8, 64, D).transpose(1, 0, 2, 3).reshape(D, D)
    wcp = f(conv_proj)[0]
    wga = w_in[:, og:og + 1024]
    wgb = w_in[:, og + 1024:og + 2048]
    wmerge = np.ascontiguousarray(np.stack([wap.reshape(D, 8, 128), wcp.reshape(D, 8, 128), wga.reshape(D, 8, 128), wgb.reshape(D, 8, 128)], axis=2).reshape(D, 8, 512))

    vecs = np.zeros((128, NV), np.float32)
    for off, v in ((V_FFN1, ffn1_norm), (V_MIX, mix_norm), (V_FFN2, ffn2_norm), (V_PLEN, ple_norm), (V_PLEG, ple_gate_norm),
                   (V_LNG, conv_ln_g), (V_LNB, conv_ln_b), (V_DWB, dw_b)):
        vecs[:, off:off + 8] = _col(f(v)[0])
    vecs[:, V_GQ] = np.tile(f(q_norm)[0], 2)
    vecs[:, V_GK] = np.tile(f(k_norm)[0], 2)
    vecs[0:64, V_MA] = 1.0
    vecs[64:128, V_MB] = 1.0
    dw = f(dw_w)[0]
    vecs[:, V_DW:NV] = dw.reshape(31, 8, 128).transpose(2, 1, 0).reshape(128, 8 * 31)
    sk = f(sinks)[0]
    sinks_l = np.full((128, 4), -1.0e4, np.float32)
    sinks_l[0:4, :] = sk.reshape(4, 4).T
    cmat, maskb, rind = _host_consts()

    shared = dict(wgu1=gu_layout(ffn1_wgu), wgu2=gu_layout(ffn2_wgu), wdn1=f(ffn1_wdown)[0], wdn2=f(ffn2_wdown)[0],
                  wconv=wconv, wq=wq, wkv=wkv, wmerge=wmerge, wout=f(w_out)[0], wple=f(ple_proj)[0], wpg=f(ple_gate_w)[0],
                  vecs=vecs, sinks_l=sinks_l, cmat=cmat)
    in_maps = []
    for core in range(NCORES):
        b, hf = core // 2, core % 2
        s0 = hf * TOK
        xt = np.zeros((D, HALO + TOK), np.float32)
        xt[:, HALO:] = x[b, s0:s0 + TOK].T
        if hf == 1:
            xt[:, :HALO] = x[b, s0 - HALO:s0].T
        pt = np.ascontiguousarray(p[b, s0:s0 + TOK].T)
        pos = np.arange(s0 - HALO, s0 + TOK)
        C, Sn = _rope_tables(pos)
        cs = np.ascontiguousarray(np.stack([C, Sn], axis=1))
        mask_halo = maskb if hf == 1 else np.full_like(maskb, NEG)
        cmask = np.ascontiguousarray(np.stack([maskb, mask_halo, rind], axis=1))
        d = dict(shared)
        d.update(xT=xt, pT=pt, cs=cs, cmask=cmask)
        in_maps.append(d)

    key = bool(_dbg)
    if key not in _NC_CACHE:
        _NC_CACHE[key] = build_program(dbg=_dbg)
    nc = _NC_CACHE[key]
    res = run_bass_kernel_spmd(nc, in_maps, core_ids=list(range(NCORES)))
    out = np.empty((BATCH, SEQ, D), np.float32)
    for core in range(NCORES):
        b, hf = core // 2, core % 2
        out[b, hf * TOK:(hf + 1) * TOK] = res.results[core]["outT"].T
    if _dbg:
        return out, res
    return out
```
